# Optimizing a Trainium2 kernel written in Bass

```python
import math
import jax, jax.numpy as jnp
from jax import lax
import numpy as np

D_MODEL = 1024
BATCH = 4
SEQ = 4096
DEPTH = 1

POOL_WINDOWS = (2, 4, 8, 16)
POOL_GROUPS = len(POOL_WINDOWS)
POOL_GROUP_DIM = D_MODEL // 8
POOL_WIDTH = POOL_GROUPS * POOL_GROUP_DIM
DN_HEAD_DIM = 128
DN_HEADS = D_MODEL // 128
DN_WIDTH = DN_HEADS * DN_HEAD_DIM
CONV_WIDTH = 4
CHUNK = 64
FFN_HIDDEN = ((8 * D_MODEL // 3) + 127) // 128 * 128
N_SUBLAYERS = 3
RMS_EPS = 1e-6
L2_EPS = 1e-6
MIX_IN_SIZES = (POOL_WIDTH, DN_WIDTH, DN_WIDTH, DN_WIDTH, DN_WIDTH, DN_HEADS, DN_HEADS, D_MODEL, D_MODEL)
MIX_IN_WIDTH = int(sum(MIX_IN_SIZES))
MIX_IN_SPLITS = tuple(int(s) for s in np.cumsum(MIX_IN_SIZES)[:-1])

kernel_name = "hybrid_pool_deltanet_macaron_adaln"


def rms_norm(x, g):
    xf = x.astype(jnp.float32)
    xf = xf * lax.rsqrt(jnp.mean(xf * xf, axis=-1, keepdims=True) + RMS_EPS)
    return (xf * g.astype(jnp.float32)).astype(x.dtype)


def l2_normalize(x):
    return x * lax.rsqrt(jnp.sum(x * x, axis=-1, keepdims=True) + L2_EPS)


def modulate(h, shift, scale):
    return h * (1.0 + scale[:, None, :]) + shift[:, None, :]


def swiglu(h, w_in, w_out):
    gate, up = jnp.split(h @ w_in, 2, axis=-1)
    return (jax.nn.silu(gate) * up) @ w_out


def causal_depthwise_conv(x, w):
    C = x.shape[-1]
    return lax.conv_general_dilated(
        x, w[:, None, :].astype(x.dtype), window_strides=(1,),
        padding=[(CONV_WIDTH - 1, 0)],
        dimension_numbers=('NWC', 'WIO', 'NWC'), feature_group_count=C)


def multiscale_pool(xp):
    T = xp.shape[1]
    xf = xp.astype(jnp.float32)
    cs = jnp.cumsum(xf, axis=1)
    pos = jnp.arange(1, T + 1, dtype=jnp.float32)[:, None]
    outs = []
    for gi, w in enumerate(POOL_WINDOWS):
        sl = slice(gi * POOL_GROUP_DIM, (gi + 1) * POOL_GROUP_DIM)
        c_g = cs[..., sl]
        lagged = jnp.pad(c_g, ((0, 0), (w, 0), (0, 0)))[:, :T]
        mean = (c_g - lagged) / jnp.minimum(pos, float(w))
        outs.append(mean - xf[..., sl])
    return jnp.stack(outs, axis=2).astype(xp.dtype)


def gated_delta_rule_chunked(q, k, v, g, beta):
    B, T, H, K = q.shape
    V = v.shape[-1]
    N = T // CHUNK

    def to_chunks(a):
        a = a.reshape((B, N, CHUNK, H) + a.shape[3:])
        return jnp.moveaxis(a, (1, 3), (0, 2))

    qc, kc, vc, bc = to_chunks(q), to_chunks(k), to_chunks(v), to_chunks(beta)
    gc = jnp.cumsum(to_chunks(g), axis=-1)
    causal = jnp.tril(jnp.ones((CHUNK, CHUNK), dtype=bool))
    strict = jnp.tril(jnp.ones((CHUNK, CHUNK), dtype=bool), -1)
    decay = jnp.exp(jnp.where(causal, gc[..., :, None] - gc[..., None, :], -jnp.inf))
    kk = jnp.einsum('nbhik,nbhjk->nbhij', kc, kc)
    a_mat = jnp.where(strict, bc[..., :, None] * kk * decay, 0.0)
    eye = jnp.eye(CHUNK, dtype=q.dtype)
    rhs = jnp.concatenate([bc[..., None] * vc, (bc * jnp.exp(gc))[..., None] * kc], axis=-1)
    sol = lax.linalg.triangular_solve(a_mat + eye, rhs, left_side=True, lower=True,
                                      unit_diagonal=True)
    u, w = sol[..., :V], sol[..., V:]
    qk = jnp.where(causal, jnp.einsum('nbhik,nbhjk->nbhij', qc, kc) * decay, 0.0)
    q_dec = qc * jnp.exp(gc)[..., None]
    g_last = gc[..., -1]
    k_dec = kc * jnp.exp(g_last[..., None] - gc)[..., None]

    def step(S, xs):
        q_i, qk_i, u_i, w_i, k_i, gl_i = xs
        v_new = u_i - jnp.einsum('bhck,bhkv->bhcv', w_i, S)
        o_i = jnp.einsum('bhck,bhkv->bhcv', q_i, S) + jnp.einsum('bhij,bhjv->bhiv', qk_i, v_new)
        S = S * jnp.exp(gl_i)[..., None, None] + jnp.einsum('bhck,bhcv->bhkv', k_i, v_new)
        return S, o_i

    S0 = jnp.zeros((B, H, K, V), dtype=q.dtype)
    _, o = lax.scan(step, S0, (q_dec, qk, u, w, k_dec, g_last))
    return jnp.moveaxis(o, (0, 2), (1, 3)).reshape(B, T, H, V)


def token_mixer(h, mix_w_in, conv_w, a_log, dt_bias, dn_norm_g, pool_w, pool_scale,
                pool_proj, dn_proj, mix_w_out):
    B, T, _ = h.shape
    proj = h @ mix_w_in
    xp, q, k, v, z, b_raw, a_raw, g_pool, g_dn = jnp.split(proj, MIX_IN_SPLITS, axis=-1)

    pooled = multiscale_pool(xp)
    ya = jnp.einsum('btgc,gcd->btgd', pooled, pool_w).reshape(B, T, POOL_WIDTH) * pool_scale
    ya = ya @ pool_proj

    qkv = jax.nn.silu(causal_depthwise_conv(jnp.concatenate([q, k, v], axis=-1), conv_w))
    qkv = qkv.astype(jnp.float32).reshape(B, T, 3, DN_HEADS, DN_HEAD_DIM)
    qh = l2_normalize(qkv[:, :, 0]) * (DN_HEAD_DIM ** -0.5)
    kh = l2_normalize(qkv[:, :, 1])
    vh = qkv[:, :, 2]
    beta = jax.nn.sigmoid(b_raw.astype(jnp.float32))
    g = -jnp.exp(a_log.astype(jnp.float32)) * jax.nn.softplus(
        a_raw.astype(jnp.float32) + dt_bias.astype(jnp.float32))
    o = gated_delta_rule_chunked(qh, kh, vh, g, beta)
    o = rms_norm(o, dn_norm_g).astype(h.dtype).reshape(B, T, DN_WIDTH) * jax.nn.silu(z)
    yb = o @ dn_proj

    merged = jax.nn.sigmoid(g_pool) * ya + jax.nn.sigmoid(g_dn) * yb
    return merged @ mix_w_out


def setup_inputs(seed: int = 0) -> dict:
    key = jax.random.key(seed)
    ks = jax.random.split(key, 20)
    D, L, F = D_MODEL, DEPTH, FFN_HIDDEN

    def nrm(k, shape, fan_in):
        return jax.random.normal(k, shape, jnp.float32) * fan_in ** -0.5

    x = jax.random.normal(ks[0], (BATCH, SEQ, D), jnp.float32)
    c = jax.random.normal(ks[1], (BATCH, D), jnp.float32)
    ada_w = 0.5 * nrm(ks[2], (L, D, N_SUBLAYERS * 3 * D), D)
    ada_b = 0.01 * jax.random.normal(ks[3], (L, N_SUBLAYERS * 3 * D), jnp.float32)
    norm_g = 1.0 + 0.05 * jax.random.normal(ks[4], (L, N_SUBLAYERS, D), jnp.float32)
    ffn1_w_in = nrm(ks[5], (L, D, 2 * F), D)
    ffn1_w_out = nrm(ks[6], (L, F, D), F)
    ffn2_w_in = nrm(ks[7], (L, D, 2 * F), D)
    ffn2_w_out = nrm(ks[8], (L, F, D), F)
    mix_w_in = nrm(ks[9], (L, D, MIX_IN_WIDTH), D)
    conv_w = nrm(ks[10], (L, CONV_WIDTH, 3 * DN_WIDTH), CONV_WIDTH)
    a_log = jnp.log(jax.random.uniform(ks[11], (L, DN_HEADS), jnp.float32, minval=1.0, maxval=16.0))
    dt = jnp.exp(jax.random.uniform(ks[12], (L, DN_HEADS), jnp.float32,
                                    minval=math.log(1e-3), maxval=math.log(1e-1)))
    dt_bias = dt + jnp.log(-jnp.expm1(-dt))
    dn_norm_g = 1.0 + 0.05 * jax.random.normal(ks[13], (L, DN_HEAD_DIM), jnp.float32)
    pool_w = nrm(ks[14], (L, POOL_GROUPS, POOL_GROUP_DIM, POOL_GROUP_DIM), POOL_GROUP_DIM)
    pool_scale = 1.0 + 0.1 * jax.random.normal(ks[15], (L, POOL_WIDTH), jnp.float32)
    pool_proj = nrm(ks[16], (L, POOL_WIDTH, D), POOL_WIDTH)
    dn_proj = nrm(ks[17], (L, DN_WIDTH, D), DN_WIDTH)
    mix_w_out = nrm(ks[18], (L, D, D), D)
    final_g = 1.0 + 0.05 * jax.random.normal(ks[19], (D,), jnp.float32)
    return {"x": x, "c": c, "ada_w": ada_w, "ada_b": ada_b, "norm_g": norm_g,
            "ffn1_w_in": ffn1_w_in, "ffn1_w_out": ffn1_w_out,
            "ffn2_w_in": ffn2_w_in, "ffn2_w_out": ffn2_w_out,
            "mix_w_in": mix_w_in, "conv_w": conv_w, "a_log": a_log, "dt_bias": dt_bias,
            "dn_norm_g": dn_norm_g, "pool_w": pool_w, "pool_scale": pool_scale,
            "pool_proj": pool_proj, "dn_proj": dn_proj, "mix_w_out": mix_w_out,
            "final_g": final_g}


def reference(x, c, ada_w, ada_b, norm_g, ffn1_w_in, ffn1_w_out, ffn2_w_in, ffn2_w_out,
              mix_w_in, conv_w, a_log, dt_bias, dn_norm_g, pool_w, pool_scale, pool_proj,
              dn_proj, mix_w_out, final_g):
    B = x.shape[0]
    for l in range(DEPTH):
        mod = (jax.nn.silu(c) @ ada_w[l] + ada_b[l]).reshape(B, N_SUBLAYERS, 3, D_MODEL)
        shift, scale, gate = mod[:, :, 0], mod[:, :, 1], mod[:, :, 2]

        h = modulate(rms_norm(x, norm_g[l, 0]), shift[:, 0], scale[:, 0])
        x = x + 0.5 * gate[:, 0, None, :] * swiglu(h, ffn1_w_in[l], ffn1_w_out[l])

        h = modulate(rms_norm(x, norm_g[l, 1]), shift[:, 1], scale[:, 1])
        x = x + gate[:, 1, None, :] * token_mixer(
            h, mix_w_in[l], conv_w[l], a_log[l], dt_bias[l], dn_norm_g[l], pool_w[l],
            pool_scale[l], pool_proj[l], dn_proj[l], mix_w_out[l])

        h = modulate(rms_norm(x, norm_g[l, 2]), shift[:, 2], scale[:, 2])
        x = x + 0.5 * gate[:, 2, None, :] * swiglu(h, ffn2_w_in[l], ffn2_w_out[l])
    return rms_norm(x, final_g)
```

```python
import numpy as np
import ml_dtypes
import concourse.bass as bass
import concourse.mybir as mybir
from concourse.bass_utils import run_bass_kernel_spmd

F32 = mybir.dt.float32
BF16 = mybir.dt.bfloat16
AF = mybir.ActivationFunctionType
ALU = mybir.AluOpType

P = 128
D = 1024
KC = 8
T = 2048
TH = 16
TE = T + TH
FH = 2816
NF = 22
MIXW = 6672
TT = [(0, TH)] + [(TH + 512 * i, 512) for i in range(4)]
NCORES = 8
CC_INC = 1


class Op:
    __slots__ = ("eng", "fn", "deps", "sig", "cnt", "dma", "dsem", "dval", "prev_dma", "idx", "dinc")


class Sched:
    ENGS = ("sp", "pe", "act", "dve", "pool")

    def __init__(self, nds=40):
        self.ops = []
        self.last_w = {}
        self.readers = {}
        self.nds = nds
        self.ndma = 0
        self.dma_ops = []
        self.bar_deps = set()
        self.bar_seen = set(self.ENGS)
        self.last_eng = {}
        self.dma_since_bar = []
        self.dtot = {}

    def barrier(self):
        self.bar_deps = set(self.last_eng.values()) | set(self.dma_since_bar)
        self.bar_seen = set()
        self.dma_since_bar = []

    @staticmethod
    def _expand(keys, is_write):
        out = []
        for k in keys:
            if isinstance(k, tuple) and len(k) == 2 and isinstance(k[0], str) and k[0].startswith("ps") \
                    and k[0] != "ps" and k[0][2:].isdigit():
                k = ("ps", int(k[0][2:]))
            if k not in out:
                out.append(k)
        return out

    def add(self, eng, fn, reads=(), writes=(), dma=False, inc=16):
        reads = self._expand(list(reads), False)
        writes = self._expand(list(writes), True)
        psr = [k for k in reads if isinstance(k, tuple) and k[0] == "ps"]
        if psr:
            reads = [k for k in reads if k not in psr]
            writes = writes + [k for k in psr if k not in writes]
        op = Op()
        op.eng = eng; op.fn = fn; op.sig = False; op.cnt = 0; op.dma = dma
        op.idx = len(self.ops)
        op.prev_dma = None
        deps = set()
        for r in reads:
            w = self.last_w.get(r)
            if w is not None:
                deps.add(w)
        for w_ in writes:
            w = self.last_w.get(w_)
            if w is not None:
                deps.add(w)
            for rd in self.readers.get(w_, ()):
                deps.add(rd)
        if eng not in self.bar_seen:
            deps |= self.bar_deps
            self.bar_seen.add(eng)
        deps.discard(op)
        op.deps = deps
        if dma:
            self.dma_since_bar.append(op)
        else:
            self.last_eng[eng] = op
        for r in reads:
            rs = self.readers.setdefault(r, set())
            if not dma:
                for o in [o for o in rs if (not o.dma) and o.eng == eng]:
                    rs.discard(o)
            rs.add(op)
        for w_ in writes:
            self.last_w[w_] = op
            self.readers[w_] = set()
        if dma:
            i = self.ndma
            self.ndma += 1
            op.dsem = i % self.nds
            self.dtot[op.dsem] = self.dtot.get(op.dsem, 0) + inc
            op.dval = self.dtot[op.dsem]
            op.dinc = inc
            if i >= self.nds:
                op.prev_dma = self.dma_ops[i - self.nds]
            self.dma_ops.append(op)
        self.ops.append(op)
        return op

    def finalize(self):
        for op in self.ops:
            for d in op.deps:
                if d.dma:
                    continue
                if d.eng == "pe" and op.eng == "pe" and not op.dma:
                    continue
                d.sig = True
        cnt = {e: 0 for e in self.ENGS}
        for op in self.ops:
            if op.dma:
                continue
            if op.sig:
                cnt[op.eng] += 1
                op.cnt = cnt[op.eng]
        return cnt

    def emit(self, eng, e, sems, dsems, final_wait_all_dma=False):
        seen = {}
        n_wait = 0
        for op in self.ops:
            if op.eng != eng:
                continue
            waits = []
            for d in op.deps:
                if d.dma:
                    waits.append((("d", d.dsem), d.dval))
                else:
                    if d.eng == "pe" and eng == "pe" and not op.dma:
                        continue
                    waits.append((("e", d.eng), d.cnt))
            if op.dma and op.prev_dma is not None:
                waits.append((("d", op.prev_dma.dsem), op.prev_dma.dval))
            best = {}
            for k, v in waits:
                if v > best.get(k, 0):
                    best[k] = v
            for k, v in best.items():
                if seen.get(k, 0) >= v:
                    continue
                seen[k] = v
                s = dsems[k[1]] if k[0] == "d" else sems[k[1]]
                e.wait_ge(s, v)
                n_wait += 1
            ins = op.fn(e)
            if op.dma:
                ins.then_inc(dsems[op.dsem], op.dinc)
            elif op.sig:
                ins.then_inc(sems[eng], 1)
        if final_wait_all_dma:
            last = {}
            for op in self.dma_ops:
                last[op.dsem] = max(last.get(op.dsem, 0), op.dval)
            for k, v in last.items():
                e.wait_ge(dsems[k], v)
        return n_wait


class Builder:
    def __init__(self, debug=None):
        self.nc = bass.Bass("TRN2", target_bir_lowering=False)
        self.s = Sched()
        self.debug = debug or {}
        self.dram = {}
        self.nps = 0

    def din(self, name, shape, dt=F32):
        t = self.nc.dram_tensor(name, list(shape), dt, kind="ExternalInput")
        self.dram[name] = t
        return t.ap()

    def dout(self, name, shape, dt=F32):
        t = self.nc.dram_tensor(name, list(shape), dt, kind="ExternalOutput")
        self.dram[name] = t
        return t.ap()

    def sb(self, name, shape, dt=F32):
        return self.nc.alloc_sbuf_tensor(name, list(shape), dt)

    def ps(self, name, shape=(P, 512), dt=F32):
        return self.nc.alloc_psum_tensor(name, list(shape), dt)

    def dma(self, out, in_, reads=(), writes=(), q="sp", **kw):
        return self.s.add(q, lambda e: e.dma_start(out=out, in_=in_, **kw), reads, writes, dma=True)

    def mm(self, out, lhsT, rhs, start=True, stop=True, reads=(), writes=()):
        return self.s.add("pe", lambda e: e.matmul(out, lhsT, rhs, start=start, stop=stop), reads, writes)

    def tr(self, out, in_, ident, reads=(), writes=()):
        return self.s.add("pe", lambda e: e.transpose(out, in_, ident), reads, writes)

    def act(self, out, in_, func, bias=None, scale=None, reads=(), writes=(), accum_out=None):
        kw = {}
        if bias is not None:
            kw["bias"] = bias
        if scale is not None:
            kw["scale"] = scale
        if accum_out is not None:
            kw["accum_out"] = accum_out
        return self.s.add("act", lambda e: e.activation(out, in_, func, **kw), reads, writes)

    def tt(self, eng, out, in0, in1, op, reads=(), writes=()):
        return self.s.add(eng, lambda e: e.tensor_tensor(out, in0, in1, op), reads, writes)

    def ts(self, eng, out, in0, s1, s2, op0, op1=None, reads=(), writes=()):
        if op1 is None:
            return self.s.add(eng, lambda e: e.tensor_scalar(out, in0, s1, None, op0), reads, writes)
        return self.s.add(eng, lambda e: e.tensor_scalar(out, in0, s1, s2, op0, op1), reads, writes)

    def stt(self, out, in0, scalar, in1, op0, op1, reads=(), writes=()):
        return self.s.add("dve", lambda e: e.scalar_tensor_tensor(out, in0, scalar, in1, op0, op1), reads, writes)

    def cp(self, eng, out, in_, reads=(), writes=()):
        if eng == "act":
            return self.s.add("act", lambda e: e.activation(out, in_, AF.Copy), reads, writes)
        return self.s.add(eng, lambda e: e.tensor_copy(out, in_), reads, writes)

    def memset(self, eng, ap, val, writes=()):
        return self.s.add(eng, lambda e: e.memset(ap, val), (), writes)


def finish(B):
    nc = B.nc
    s = B.s
    cnt = s.finalize()
    engmap = {"sp": "sync", "pe": "tensor", "act": "scalar", "dve": "vector", "pool": "gpsimd"}
    from contextlib import ExitStack
    with ExitStack() as es:
        sems = {e: es.enter_context(nc.semaphore(f"sem_{e}")) for e in Sched.ENGS}
        dsems = [es.enter_context(nc.semaphore(f"dsem{i}")) for i in range(s.nds)]
        block = es.enter_context(nc.Block())
        stats = {}

        @block.sync
        def _(e):
            stats["sp"] = s.emit("sp", e, sems, dsems, final_wait_all_dma=True)

        @block.tensor
        def _(e):
            stats["pe"] = s.emit("pe", e, sems, dsems)

        @block.scalar
        def _(e):
            stats["act"] = s.emit("act", e, sems, dsems)

        @block.vector
        def _(e):
            stats["dve"] = s.emit("dve", e, sems, dsems)

        @block.gpsimd
        def _(e):
            stats["pool"] = s.emit("pool", e, sems, dsems)
    B.stats = (cnt, stats, len(s.ops))
    return B


def build(debug=None):
    B = Builder(debug)
    nc = B.nc
    s = B.s
    dbg = B.debug

    x_ext = B.din("x_ext", [TE, D])
    cT_d = B.din("cT", [P, KC])
    ada_w = B.din("ada_w", [D, 9 * D])
    ada_bT_d = B.din("ada_bT", [P, 72])
    norm_gT_d = B.din("norm_gT", [P, 32])
    ident_d = B.din("ident", [P, P])
    w1_in = B.din("ffn1_w_in", [D, 2 * FH])
    w1_out = B.din("ffn1_w_out", [FH, D])
    w2_in = B.din("ffn2_w_in", [D, 2 * FH])
    w2_out = B.din("ffn2_w_out", [FH, D])
    out_d = B.dout("out", [T, D])

    xT = B.sb("xT", [P, KC, TE], F32)
    hT = B.sb("hT", [P, KC, TE], BF16)
    ident = B.sb("ident_sb", [P, P], F32)
    ones_bf = B.sb("ones_bf", [P, P], BF16)
    cT = B.sb("cTs", [P, KC], F32)
    scT = B.sb("scT", [P, KC], F32)
    ada_bT = B.sb("ada_bTs", [P, 72], F32)
    modT = B.sb("modT", [P, 72], F32)
    norm_gT = B.sb("norm_gTs", [P, 32], F32)
    gsT = B.sb("gsT", [P, 24], F32)
    hgT = B.sb("hgT", [P, 24], F32)
    small = B.sb("small", [P, 2048], F32)
    NREG = (nc.sbuf_bytes_remaining - 2048 - 4096 - 4096 - 512) // 4 // 8 * 8
    REG = B.sb("reg", [P, NREG], F32)
    scr = REG[:, 0:6144]

    class Carver:
        def __init__(self, off=0):
            self.off = off

        def f32(self, n):
            ap = REG[:, self.off:self.off + n]
            self.off += n
            assert self.off <= NREG, (self.off, NREG)
            return ap

        def bf16(self, n):
            m = (n + 1) // 2
            ap = REG[:, self.off:self.off + m].bitcast(BF16)[:, 0:n]
            self.off += m
            assert self.off <= NREG, (self.off, NREG)
            return ap
    xin = [scr[:, i * 1024:(i + 1) * 1024] for i in range(2)]
    sq = B.sb("sq", [P, 2, 512], BF16)
    rstd = [B.sb(f"rstd{i}", [P, 512], F32) for i in range(2)]
    tmpf = [B.sb(f"tmpf{i}", [P, 512], F32) for i in range(2)]

    psb = [B.ps(f"psb{i}") for i in range(8)]

    B.dma(ident[:], ident_d, writes=["ident"])
    B.memset("pool", ones_bf[:], 1.0, writes=["ones"])
    B.dma(cT[:], cT_d, writes=["cT"])
    B.dma(ada_bT[:], ada_bT_d, writes=["ada_bT"])
    B.dma(norm_gT[:], norm_gT_d, writes=["norm_gT"])

    rowtiles = [(0, TH)] + [(TH + 128 * i, 128) for i in range(16)]
    for ti, (r0, n) in enumerate(rowtiles):
        xb = xin[ti % 2]
        B.dma(xb[0:n, :], x_ext[r0:r0 + n, :], writes=[("xin", ti % 2), ("scr", ti % 2)])
        for half in range(2):
            bank = psb[(2 * ti + half) % 4]
            bkey = ("ps", (2 * ti + half) % 4)
            for j in range(4):
                k = half * 4 + j
                B.tr(bank[:, j * 128:j * 128 + n], xb[0:n, k * 128:(k + 1) * 128], ident[0:n, 0:n],
                     reads=[("xin", ti % 2), "ident"], writes=[bkey])
            src = bank[:].rearrange("p (j c) -> p j c", j=4)[:, :, 0:n]
            B.cp("act" if half == 0 else "dve", xT[:, half * 4:half * 4 + 4, r0:r0 + n], src,
                 reads=[bkey], writes=[("xT", ti)])
    XT_ALL = [("xT", ti) for ti in range(len(rowtiles))]

    def xkeys(c0, n):
        ks = []
        for ti, (r0, m) in enumerate(rowtiles):
            if r0 < c0 + n and c0 < r0 + m:
                ks.append(("xT", ti))
        return ks

    B.act(scT[:], cT[:], AF.Silu, reads=["cT"], writes=["scT"])
    AW = 256
    adaw_st = [scr[:, i * 2048:(i + 1) * 2048].rearrange("p (k c) -> p k c", k=KC) for i in range(3)]
    adaw_keys = [[("scr", 0), ("scr", 1)], [("scr", 2)], [("scr", 3)]]
    nblk = 9 * D // AW
    NB0 = 24 * 128 // AW

    def ada_dma(blk):
        st = adaw_st[(blk + 1) % 3]
        for k in range(KC):
            B.dma(st[:, k, :], ada_w[k * 128:(k + 1) * 128, blk * AW:(blk + 1) * AW],
                  writes=[("adaw", (blk + 1) % 3, k)])

    def ada_mm(blk):
        st = adaw_st[(blk + 1) % 3]
        for jj in range(AW // 128):
            j = blk * (AW // 128) + jj
            for k in range(KC):
                B.mm(psb[7][:, j:j + 1], st[:, k, jj * 128:(jj + 1) * 128], scT[:, k:k + 1],
                     start=(k == 0), stop=(k == KC - 1),
                     reads=[("adaw", (blk + 1) % 3, k), "scT"], writes=[("ps", 7)])

    def ada_finish(sls):
        for sl in sls:
            cs = slice(sl * 24, (sl + 1) * 24)
            B.tt("dve", modT[:, cs], psb[7][:, cs], ada_bT[:, cs], ALU.add, reads=[("ps", 7), "ada_bT"], writes=["modT"])
            sc = modT[:, (sl * 3 + 1) * 8:(sl * 3 + 2) * 8]
            gt = modT[:, (sl * 3 + 2) * 8:(sl * 3 + 3) * 8]
            s.add("dve", lambda e, sc=sc, sl=sl: e.scalar_tensor_tensor(
                gsT[:, sl * 8:(sl + 1) * 8], sc, 1.0, norm_gT[:, sl * 8:(sl + 1) * 8], ALU.add, ALU.mult),
                ["modT", "norm_gT"], [("gsT", sl)])
            B.ts("dve", hgT[:, sl * 8:(sl + 1) * 8], gt, 0.5 if sl != 1 else 1.0, None, ALU.mult,
                 reads=["modT"], writes=[("hgT", sl)])

    B.memset("pool", adaw_st[0][0:1, 0, 0:1], 0.0,
             writes=[("adaw", 0, k) for k in range(KC)] + [("xin", 0), ("xin", 1)])
    for blk in range(NB0):
        ada_dma(blk)
        ada_mm(blk)
    ada_finish([0])
    ada_rest = list(range(NB0, nblk))

    def norm_tile(tt_i, gs_ap_fn, gs_keys, out_fn, shift_fn=None):
        c0, n = TT[tt_i]
        xk = xkeys(c0, n)
        rs = rstd[tt_i % 2]
        rkey = ("rstd", tt_i % 2)
        for k in range(KC):
            B.act(sq[:, k % 2, 0:n], xT[:, k, c0:c0 + n], AF.Square, reads=xk, writes=[("sq", k % 2)])
            B.mm(psb[6][:, 0:n], ones_bf[:], sq[:, k % 2, 0:n], start=(k == 0), stop=(k == KC - 1),
                 reads=[("sq", k % 2), "ones"], writes=[("ps", 6)])
        B.act(rs[:, 0:n], psb[6][:, 0:n], AF.Ln, bias=eps_ap, scale=1.0 / D, reads=[("ps", 6), "eps"], writes=[rkey])
        B.act(rs[:, 0:n], rs[:, 0:n], AF.Exp, scale=-0.5, reads=[rkey], writes=[rkey])
        for k in range(KC):
            tb = tmpf[k % 2]
            tkey = ("tmpf", k % 2)
            B.stt(tb[:, 0:n], xT[:, k, c0:c0 + n], gs_ap_fn(k), rs[:, 0:n], ALU.mult, ALU.mult,
                  reads=xk + [rkey] + gs_keys, writes=[tkey])
            out_fn(k, tb[:, 0:n], tkey, c0, n)

    eps_t = B.sb("eps_t", [P, 1], F32)
    B.memset("pool", eps_t[:], 1e-6, writes=["eps"])
    eps_ap = eps_t[:, 0:1]

    GROUPS = [3, 3, 3, 3, 3, 3, 2, 2]
    FG = 3
    cf = Carver(6144)
    win_st = [cf.f32(2 * FG * 128).rearrange("p (a c) -> p a c", a=2) for i in range(2)]
    wout_st = [cf.f32(D) for i in range(2)]
    win_bf = [cf.bf16(KC * 2 * FG * 128).rearrange("p (k a c) -> p k a c", k=KC, a=2) for i in range(2)]
    wout_bf = [cf.bf16(FG * D).rearrange("p (f c) -> p f c", f=FG) for i in range(2)]
    actb = [cf.bf16(FG * 512).rearrange("p (f c) -> p f c", f=FG) for i in range(2)]
    sgb = [cf.f32(512) for i in range(2)]
    st_ctr = [0, 0]

    def ffn(sl, w_in, w_out, tiles, hook=None):
        shift = lambda k: modT[:, (sl * 3) * 8 + k:(sl * 3) * 8 + k + 1]
        for tt_i in tiles:
            def out_fn(k, t_ap, tkey, c0, n):
                B.act(hT[:, k, c0:c0 + n], t_ap, AF.Identity, bias=shift(k), scale=1.0,
                      reads=[tkey, "modT"], writes=[("hT", tt_i)])
            norm_tile(tt_i, lambda k: gsT[:, sl * 8 + k:sl * 8 + k + 1], [("gsT", sl)], out_fn)
        f0s = [sum(GROUPS[:i]) for i in range(len(GROUPS))]

        def load_group(gi):
            fg = GROUPS[gi]
            f0 = f0s[gi]
            slot = gi % 2
            wb = win_bf[slot]
            ob = wout_bf[slot]
            ncol = fg * 128
            for k in range(KC):
                si = st_ctr[0] % 2
                st_ctr[0] += 1
                stg = win_st[si]
                skey = ("win_st", si)
                for gu in range(2):
                    B.dma(stg[:, gu, 0:ncol],
                          w_in[k * 128:(k + 1) * 128, gu * FH + f0 * 128:gu * FH + f0 * 128 + ncol],
                          writes=[skey])
                B.cp("pool", wb[:, k, :, 0:ncol], stg[:, :, 0:ncol],
                     reads=[skey], writes=[("win_bf", slot)])
            for ff in range(fg):
                si = st_ctr[1] % 2
                st_ctr[1] += 1
                f = f0 + ff
                B.dma(wout_st[si], w_out[f * 128:(f + 1) * 128, :], writes=[("wout_st", si)])
                B.cp("pool", ob[:, ff, :], wout_st[si],
                     reads=[("wout_st", si)], writes=[("wout_bf", slot)])

        load_group(0)
        for gi, fg in enumerate(GROUPS):
            hk = hook(gi) if hook else []
            for blk in hk:
                ada_dma(blk)
            if gi + 1 < len(GROUPS):
                load_group(gi + 1)
            slot = gi % 2
            wb = win_bf[slot]
            ob = wout_bf[slot]

            def p1(tt_i):
                c0, n = TT[tt_i]
                ab = actb[tt_i % 2]
                for ff in range(fg):
                    gb = psb[(2 * ff) % 4]
                    ub = psb[(2 * ff + 1) % 4]
                    gk = ("ps", (2 * ff) % 4)
                    uk = ("ps", (2 * ff + 1) % 4)
                    for k in range(KC):
                        B.mm(gb[:, 0:n], wb[:, k, 0, ff * 128:(ff + 1) * 128], hT[:, k, c0:c0 + n],
                             start=(k == 0), stop=(k == KC - 1),
                             reads=[("win_bf", slot), ("hT", tt_i)], writes=[gk])
                    for k in range(KC):
                        B.mm(ub[:, 0:n], wb[:, k, 1, ff * 128:(ff + 1) * 128], hT[:, k, c0:c0 + n],
                             start=(k == 0), stop=(k == KC - 1),
                             reads=[("win_bf", slot), ("hT", tt_i)], writes=[uk])
                    sg = sgb[ff % 2]
                    B.act(sg[:, 0:n], gb[:, 0:n], AF.Silu, reads=[gk], writes=[("sgb", ff % 2)])
                    B.tt("dve", ab[:, ff, 0:n], sg[:, 0:n], ub[:, 0:n], ALU.mult,
                         reads=[("sgb", ff % 2), uk], writes=[("actb", tt_i % 2)])

            def p2(tt_i):
                c0, n = TT[tt_i]
                ab = actb[tt_i % 2]
                xk = xkeys(c0, n)
                for d in range(KC):
                    yb = psb[4 + d % 2]
                    yk = ("ps", 4 + d % 2)
                    for ff in range(fg):
                        B.mm(yb[:, 0:n], ob[:, ff, d * 128:(d + 1) * 128], ab[:, ff, 0:n],
                             start=(ff == 0), stop=(ff == fg - 1),
                             reads=[("wout_bf", slot), ("actb", tt_i % 2)], writes=[yk])
                    B.stt(xT[:, d, c0:c0 + n], yb[:, 0:n], hgT[:, sl * 8 + d:sl * 8 + d + 1], xT[:, d, c0:c0 + n],
                          ALU.mult, ALU.add, reads=[yk, ("hgT", sl)] + xk, writes=xk)

            for ii, tt_i in enumerate(tiles):
                p1(tt_i)
                if ii > 0:
                    p2(tiles[ii - 1])
            p2(tiles[-1])
            for blk in hk:
                ada_mm(blk)

    mix_w = B.din("mix_w_in", [D, MIXW])
    convw_d = B.din("conv_wT", [P, 96])
    alog_d = B.din("alog_rep", [P, 128])
    dtb_d = B.din("dtb_rep", [P, 128])
    dng_d = B.din("dng", [P, 1])
    poolw_d = B.din("pool_w", [4, P, P])
    pscale_d = B.din("pool_scaleT", [P, 4])
    poolproj_d = B.din("pool_proj", [512, D])
    dnproj_d = B.din("dn_proj", [D, D])
    mixout_d = B.din("mix_w_out", [D, D])
    cst_d = B.din("cst", [P, 9 * 128])
    hmask_d = B.din("halo_mask", [P, 1])
    invdiv_d = B.din("invdiv", [P, 64])
    sx_in = nc.dram_tensor("sx_in", [8 * P, P], F32, kind="Internal").ap()
    sx_out = nc.dram_tensor("sx_out", [8 * 2 * P, P], F32, kind="Internal").ap()
    cc_sem_box = [None]

    so = [0]

    def sm(n):
        ap = small[:, so[0]:so[0] + n]
        so[0] += n
        assert so[0] <= 2048
        return ap
    convw = sm(96); alog = sm(128); dtb = sm(128); dng = sm(1); pscale = sm(4); hmask = sm(1); invdiv = sm(64)
    one_c = sm(1)
    beta = sm(128); gcs = sm(128); egc = sm(128); ekd = sm(128); eglA = sm(128); eglB = sm(128); bke = sm(128)
    gtmp = sm(128); gtmp2 = sm(128)
    B.dma(convw, convw_d, writes=["convw"]); B.dma(alog, alog_d, writes=["alog"]); B.dma(dtb, dtb_d, writes=["dtb"])
    B.dma(dng, dng_d, writes=["dng"]); B.dma(pscale, pscale_d, writes=["pscale"]); B.dma(hmask, hmask_d, writes=["hmask"])
    B.dma(invdiv, invdiv_d, writes=["invdiv"])
    B.memset("pool", one_c, 1.0, writes=["one_c"])

    def mixer():
        c = Carver(0)
        cstf = c.f32(9 * 128)
        B.dma(cstf, cst_d, writes=["cstf"])
        CM = lambda i: cstf[:, i * 128:(i + 1) * 128]
        mLs, mUi, m16, nm16, Lc, Lall, LA, LB, onesf = [CM(i) for i in range(9)]
        ident_bf = c.bf16(128)
        B.cp("pool", ident_bf, ident[:], reads=["ident"], writes=["ident_bf"])
        oT = c.bf16(8 * T).rearrange("p (h t) -> p h t", h=8)
        wst1 = c.f32(KC * 128).rearrange("p (k c) -> p k c", k=KC)
        wst = [wst1, wst1]
        wbf = [c.bf16(KC * 128).rearrange("p (k c) -> p k c", k=KC) for i in range(2)]
        mark = c.off
        wctr = [0]

        def load_w(dram, row0, col0, ncols, nk=KC):
            i = wctr[0] % 2
            wctr[0] += 1
            for k in range(nk):
                B.dma(wst[i][:, k, 0:ncols], dram[row0 + k * 128:row0 + (k + 1) * 128, col0:col0 + ncols],
                      writes=["wst"])
            B.cp("pool", wbf[i][:, 0:nk, 0:ncols], wst[i][:, 0:nk, 0:ncols], reads=["wst"], writes=[("wbf", i)])
            return wbf[i], ("wbf", i)

        def proj_gen(col0, tiles, out_fn, banks, ncols=128):
            w, wk = load_w(mix_w, 0, col0, ncols)
            yield
            for ii, tt_i in enumerate(tiles):
                c0, n = TT[tt_i]
                bi = banks[ii % len(banks)]
                for k in range(KC):
                    B.mm(psb[bi][0:ncols, 0:n], w[:, k, 0:ncols], hT[:, k, c0:c0 + n], start=(k == 0), stop=(k == KC - 1),
                         reads=[wk, ("hT", tt_i)], writes=[("ps", bi)])
                r = out_fn(tt_i, psb[bi][0:ncols, 0:n], ("ps", bi), c0, n)
                if r is not None:
                    yield from r
                yield

        def evac_masked(dst_fn, key):
            def f(tt_i, ps, pk, c0, n):
                if tt_i == 0:
                    B.act(dst_fn(c0, n), ps, AF.Copy, scale=hmask, reads=[pk, "hmask"], writes=[key])
                else:
                    B.cp("act" if tt_i % 2 else "dve", dst_fn(c0, n), ps, reads=[pk], writes=[key])
            return f

        def run(*gens):
            gens = list(gens)
            while gens:
                for g in list(gens):
                    try:
                        next(g)
                    except StopIteration:
                        gens.remove(g)

        for tt_i in range(len(TT)):
            def out_fn(k, t_ap, tkey, c0, n):
                B.act(hT[:, k, c0:c0 + n], t_ap, AF.Identity, bias=modT[:, 24 + k:24 + k + 1], scale=1.0,
                      reads=[tkey, "modT"], writes=[("hT", tt_i)])
            norm_tile(tt_i, lambda k: gsT[:, 8 + k:8 + k + 1], [("gsT", 1)], out_fn)

        wba_st = c.f32(KC * 16).rearrange("p (k c) -> p k c", k=KC)
        wba = c.bf16(KC * 16).rearrange("p (k c) -> p k c", k=KC)
        for k in range(KC):
            B.dma(wba_st[:, k, :], mix_w[k * 128:(k + 1) * 128, 4608:4624], writes=["wba_st"])
        B.cp("pool", wba, wba_st, reads=["wba_st"], writes=["wba"])
        for j in range(16):
            for k in range(KC):
                B.mm(psb[4][:, j * 16:(j + 1) * 16], hT[:, k, TH + j * 128:TH + (j + 1) * 128], wba[:, k, :],
                     start=(k == 0), stop=(k == KC - 1), reads=["wba"] + [("hT", t) for t in range(5)], writes=[("ps", 4)])
        pv = psb[4][:, 0:256].rearrange("p (j c) -> p j c", c=16)
        v3 = lambda ap: ap.rearrange("p (j h) -> p j h", h=8)
        B.act(v3(beta), pv[:, :, 0:8], AF.Sigmoid, reads=[("ps", 4)], writes=["beta"])
        B.tt("dve", v3(gtmp), pv[:, :, 8:16], v3(dtb), ALU.add, reads=[("ps", 4), "dtb"], writes=["gtmp"])
        B.act(gtmp, gtmp, AF.Exp, reads=["gtmp"], writes=["gtmp"])
        B.act(gtmp, gtmp, AF.Ln, bias=one_c, scale=1.0, reads=["gtmp", "one_c"], writes=["gtmp"])
        B.act(gtmp2, alog, AF.Exp, reads=["alog"], writes=["gtmp2"])
        B.stt(gtmp, gtmp, -1.0, gtmp2, ALU.mult, ALU.mult, reads=["gtmp", "gtmp2"], writes=["gtmp"])
        for i, Lm in enumerate((Lc, Lall, LA, LB)):
            B.mm(psb[5][:, i * 128:(i + 1) * 128], Lm, gtmp, reads=["cstf", "gtmp"], writes=[("ps", 5)])
        B.cp("act", gcs, psb[5][:, 0:128], reads=[("ps", 5)], writes=["gcs"])
        B.act(egc, psb[5][:, 0:128], AF.Exp, reads=[("ps", 5)], writes=["egc"])
        B.tt("dve", gtmp2, psb[5][:, 128:256], gcs, ALU.subtract, reads=[("ps", 5), "gcs"], writes=["gtmp2"])
        B.act(ekd, gtmp2, AF.Exp, reads=["gtmp2"], writes=["ekd"])
        B.act(eglA, psb[5][:, 256:384], AF.Exp, reads=[("ps", 5)], writes=["eglA"])
        B.act(eglB, psb[5][:, 384:512], AF.Exp, reads=[("ps", 5)], writes=["eglB"])
        B.tt("dve", bke, beta, egc, ALU.mult, reads=["beta", "egc"], writes=["bke"])

        c.off = mark
        s.barrier()
        pre = c.bf16(TE)
        dg = c.bf16(4 * 128).rearrange("p (j c) -> p j c", j=4)
        qT = c.bf16(T); kT = c.bf16(T); vT = c.bf16(T); QtT = c.bf16(T)
        Sx = c.f32(128); S0 = Sx; S0b = c.bf16(128); Smidb = c.bf16(128)
        PTA = c.bf16(128); PTB = c.bf16(128)

        class Lane:
            pass
        lanes = []
        for L in range(2):
            ln = Lane()
            ln.L = L
            ln.K = lambda name, L=L: f"{name}@{L}"
            ln.kvtok = c.bf16(256); ln.ktok = ln.kvtok[:, 0:128]; ln.vtok = ln.kvtok[:, 128:256]
            ln.gcd = c.f32(128); ln.tL = c.f32(128); ln.tU = c.f32(128); ln.egb = c.bf16(128)
            ln.Am = c.bf16(128); ln.Um = c.bf16(128); ln.EU = c.bf16(128); ln.QKm = c.bf16(128)
            ln.XY = [c.bf16(256) for i in range(2)]
            ln.Xb = [xy[:, 0:128] for xy in ln.XY]; ln.Yb = [xy[:, 128:256] for xy in ln.XY]; ln.RT = c.bf16(128)
            ln.rf = c.f32(256); ln.rb = c.bf16(256); ln.yb_ = ln.rb
            ln.uext = c.f32(256); ln.wbf_ = None; ln.wT = c.bf16(128); ln.kdec = c.bf16(128); ln.qdec = c.bf16(128)
            ln.GT = ln.Um; ln.y0wT = ln.tL
            ln.uext2 = [ln.uext, c.f32(256)]
            ln.wT2 = [ln.wT, c.bf16(128)]; ln.kdec2 = [ln.kdec, c.bf16(128)]
            ln.qdec2 = [ln.qdec, c.bf16(128)]; ln.QKm2 = [ln.QKm, c.bf16(128)]
            ln.Z = c.f32(256); ln.Zb = c.bf16(256); ln.vn = c.bf16(256)
            ln.bk = [4 * L + i for i in range(4)]
            B.memset("pool", ln.uext2[0][:, 128:256], 0.0, writes=[ln.K("uext0")])
            B.memset("pool", ln.uext2[1][:, 128:256], 0.0, writes=[ln.K("uext1")])
            lanes.append(ln)

        def h1_gen(h):
            sqb = sq[:, 0, :]; rinv = rstd[0][:]
            ksq, krinv = ("sq", 0), ("rstd", 0)
            P2, P3 = psb[2], psb[3]
            k2, k3 = ("ps", 2), ("ps", 3)
            for ci, (colbase, dstT) in enumerate(((512, qT), (1536, kT), (2560, vT))):
                chunk = ci * 8 + h
                for j in range(4):
                    B.act(dg[:, j, :], ident_bf, AF.Copy, scale=convw[:, chunk * 4 + j:chunk * 4 + j + 1],
                          reads=["ident_bf", "convw"], writes=["dg"])
                yield

                def conv_chain(tt_i, ci=ci, dstT=dstT):
                    c0, n = TT[tt_i]
                    for j in range(4):
                        B.mm(P2[:, 0:n], dg[:, j, :], pre[:, c0 - 3 + j:c0 - 3 + j + n], start=(j == 0), stop=(j == 3),
                             reads=["dg", ("pre", tt_i - 1), ("pre", tt_i)], writes=[k2])
                    o0 = c0 - TH
                    qk = ("qkv", ci, tt_i)
                    B.act(dstT[:, o0:o0 + n], P2[:, 0:n], AF.Silu, reads=[k2], writes=[qk])
                    yield
                    if ci < 2:
                        B.act(sqb, dstT[:, o0:o0 + n], AF.Square, reads=[qk], writes=[ksq])
                        B.mm(P3[:, 0:n], ones_bf[:], sqb, reads=[ksq, "ones"], writes=[k3])
                        yield
                        B.act(rinv, P3[:, 0:n], AF.Ln, bias=eps_ap, scale=1.0, reads=[k3, "eps"], writes=[krinv])
                        B.act(rinv, rinv, AF.Exp, scale=-0.5, reads=[krinv], writes=[krinv])
                        yield
                        B.stt(dstT[:, o0:o0 + n], dstT[:, o0:o0 + n], (128.0 ** -0.5) if ci == 0 else 1.0, rinv,
                              ALU.mult, ALU.mult, reads=[qk, krinv], writes=[qk])
                        yield

                def out_fn(tt_i, ps, pk, c0, n):
                    if tt_i == 0:
                        B.act(pre[:, c0:c0 + n], ps, AF.Copy, scale=hmask, reads=[pk, "hmask"], writes=[("pre", 0)])
                        return None
                    B.cp("act" if tt_i % 2 else "dve", pre[:, c0:c0 + n], ps, reads=[pk], writes=[("pre", tt_i)])
                    return conv_chain(tt_i)
                yield from proj_gen(colbase + h * 128, range(len(TT)), out_fn, [0, 1])

        def prep_tile(h, ln, j, jlo, jhi):
            K = ln.K
            P0, P1, P2, P3 = [psb[b] for b in ln.bk]
            k0, k1, k2, k3 = [("ps", b) for b in ln.bk]
            ktok, vtok, gcd, tL, tU, egb = ln.ktok, ln.vtok, ln.gcd, ln.tL, ln.tU, ln.egb
            DL, DU = tL, tU
            Am, Um, EU, QKm, Xb, Yb, RT = ln.Am, ln.Um, ln.EU, ln.QKm, ln.Xb, ln.Yb, ln.RT
            rf, rb, yb_, uext, wbf_, wT, kdec, qdec = ln.rf, ln.rb, ln.yb_, ln.uext, ln.wbf_, ln.wT, ln.kdec, ln.qdec
            Z, Zb, vn = ln.Z, ln.Zb, ln.vn
            bsel = j % 2
            uext, wT, kdec, qdec, QKm = ln.uext2[bsel], ln.wT2[bsel], ln.kdec2[bsel], ln.qdec2[bsel], ln.QKm2[bsel]
            Kb = lambda name: K(f"{name}{bsel}")
            if j == jlo:
                B.ts("pool", gcd, ident[:], gcs[:, j * 8 + h:j * 8 + h + 1], None, ALU.mult,
                     reads=["ident", "gcs"], writes=[K("gcd")])
            col = j * 8 + h
            t0 = j * 128
            tt_i = 1 + j // 4
            qk_keys = [("qkv", 0, tt_i), ("qkv", 1, tt_i), ("qkv", 2, tt_i)]
            kTt = kT[:, t0:t0 + 128]; qTt = qT[:, t0:t0 + 128]; vTt = vT[:, t0:t0 + 128]
            gp = gcs[:, col:col + 1]
            B.mm(P0[:, 0:128], kTt, ident_bf, reads=qk_keys + ["ident_bf"], writes=[k0])
            B.mm(P0[:, 128:256], vTt, ident_bf, reads=qk_keys + ["ident_bf"], writes=[k0])
            yield
            B.cp("act", ln.kvtok, P0[:, 0:256], reads=[k0], writes=[K("ktok"), K("vtok")])
            B.mm(P1[:, 0:128], onesf, gcd, reads=["cstf", K("gcd")], writes=[k1])
            if j + 1 < jhi:
                ncol = (j + 1) * 8 + h
                B.ts("pool", gcd, ident[:], gcs[:, ncol:ncol + 1], None, ALU.mult,
                     reads=["ident", "gcs"], writes=[K("gcd")])
            B.mm(P1[:, 128:256], kTt, kTt, reads=qk_keys, writes=[k1])
            B.mm(P1[:, 256:384], kTt, qTt, reads=qk_keys, writes=[k1])
            yield
            B.stt(tL, P1[:, 0:128], gp, mLs, ALU.subtract, ALU.max, reads=[k1, "gcs", "cstf"], writes=[K("tL")])
            B.stt(tU, P1[:, 0:128], gp, mUi, ALU.subtract, ALU.min, reads=[k1, "gcs", "cstf"], writes=[K("tU")])
            yield
            B.act(egb, P1[:, 0:128], AF.Exp, reads=[k1], writes=[K("egb")])
            B.act(DL, tL, AF.Exp, scale=-1.0, reads=[K("tL")], writes=[K("tL")])
            B.act(DU, tU, AF.Exp, reads=[K("tU")], writes=[K("tU")])
            yield
            B.stt(Am, P1[:, 128:256], beta[:, col:col + 1], DL, ALU.mult, ALU.mult,
                  reads=[k1, "beta", K("tL")], writes=[K("Am")])
            B.tt("dve", QKm, P1[:, 256:384], DU, ALU.mult, reads=[k1, K("tU")], writes=[Kb("QKm")])
            B.act(rb[:, 0:128], vtok, AF.Copy, scale=beta[:, col:col + 1], reads=[K("vtok"), "beta"], writes=[K("rb")])
            B.act(rb[:, 128:256], ktok, AF.Copy, scale=bke[:, col:col + 1], reads=[K("ktok"), "bke"], writes=[K("rb")])
            yield
            B.mm(P0[:, 256:384], Am, ident_bf, reads=[K("Am"), "ident_bf"], writes=[k0])
            X0, Y0 = Xb[0], Yb[0]
            B.stt(X0, Am, -1.0, m16, ALU.mult, ALU.mult, reads=[K("Am"), "cstf"], writes=[(K("X"), 0)])
            yield
            B.cp("act", Um, P0[:, 256:384], reads=[k0], writes=[K("Um")])
            yield
            B.stt(Y0, Um, -1.0, m16, ALU.mult, ALU.mult, reads=[K("Um"), "cstf"], writes=[(K("Y"), 0)])
            B.tt("dve", EU, Am, nm16, ALU.mult, reads=[K("Am"), "cstf"], writes=[K("EU")])
            B.stt(RT, Y0, 1.0, ident[:], ALU.mult, ALU.add, reads=[(K("Y"), 0), "ident"], writes=[K("RT")])
            yield
            cur = 0
            for lvl in range(3):
                nx = cur ^ 1
                B.mm(P0[:, 0:128], Yb[cur], Xb[cur], reads=[(K("X"), cur), (K("Y"), cur)], writes=[k0])
                if lvl < 2:
                    B.mm(P0[:, 128:256], Xb[cur], Yb[cur], reads=[(K("X"), cur), (K("Y"), cur)], writes=[k0])
                yield
                if lvl < 2:
                    B.cp("act", ln.XY[nx], P0[:, 0:256], reads=[k0], writes=[(K("X"), nx), (K("Y"), nx)])
                else:
                    B.cp("act", Xb[nx], P0[:, 0:128], reads=[k0], writes=[(K("X"), nx)])
                yield
                B.mm(P1[:, 0:128], Xb[nx], RT, reads=[(K("X"), nx), K("RT")], writes=[k1])
                yield
                B.tt("dve", RT, RT, P1[:, 0:128], ALU.add, reads=[K("RT"), k1], writes=[K("RT")])
                yield
                cur = nx
            GT, y0wT = ln.GT, ln.y0wT
            B.mm(P0[:, 0:256], RT, rb, reads=[K("RT"), K("rb")], writes=[k0])
            B.mm(P1[:, 0:128], EU, RT, reads=[K("EU"), K("RT")], writes=[k1])
            B.mm(P1[:, 128:256], rb[:, 128:256], RT, reads=[K("RT"), K("rb")], writes=[k1])
            B.ts("pool", kdec, ktok, ekd[:, col:col + 1], None, ALU.mult, reads=[K("ktok"), "ekd"], writes=[Kb("kdec")])
            yield
            B.cp("act", yb_, P0[:, 0:256], reads=[k0], writes=[K("rb")])
            B.cp("act", GT, P1[:, 0:128], reads=[k1], writes=[K("Um")])
            B.cp("dve", rf, P0[:, 0:256], reads=[k0], writes=[K("rf")])
            B.cp("act", y0wT, P1[:, 128:256], reads=[k1], writes=[K("tL")])
            B.tt("dve", qdec, qTt, egb, ALU.mult, reads=qk_keys + [K("egb")], writes=[Kb("qdec")])
            yield
            for it in range(3):
                if it < 2:
                    B.mm(P0[:, 0:256], GT, yb_, reads=[K("Um"), K("rb")], writes=[k0])
                    yield
                    B.tt("dve", yb_, rf, P0[:, 0:256], ALU.subtract, reads=[K("rf"), k0], writes=[K("rb")])
                    yield
                else:
                    B.mm(P0[:, 0:128], GT, yb_[:, 0:128], reads=[K("Um"), K("rb")], writes=[k0])
                    B.mm(P0[:, 128:256], yb_[:, 128:256], GT, reads=[K("Um"), K("rb")], writes=[k0])
                    yield
                    B.tt("dve", wT, y0wT, P0[:, 128:256], ALU.subtract, reads=[K("tL"), k0], writes=[Kb("wT")])
                    B.tt("dve", uext[:, 0:128], rf[:, 0:128], P0[:, 0:128], ALU.subtract,
                         reads=[K("rf"), k0], writes=[Kb("uext")])
                    yield

        def recur_tile(h, ln, j, jlo, jhi):
            K = ln.K
            P0, P1, P2, P3 = [psb[b] for b in ln.bk]
            k0, k1, k2, k3 = [("ps", b) for b in ln.bk]
            ktok, vtok, gcd, tL, tU, egb = ln.ktok, ln.vtok, ln.gcd, ln.tL, ln.tU, ln.egb
            DL, DU = tL, tU
            Am, Um, EU, QKm, Xb, Yb, RT = ln.Am, ln.Um, ln.EU, ln.QKm, ln.Xb, ln.Yb, ln.RT
            rf, rb, yb_, uext, wbf_, wT, kdec, qdec = ln.rf, ln.rb, ln.yb_, ln.uext, ln.wbf_, ln.wT, ln.kdec, ln.qdec
            Z, Zb, vn = ln.Z, ln.Zb, ln.vn
            bsel = j % 2
            uext, wT, kdec, qdec, QKm = ln.uext2[bsel], ln.wT2[bsel], ln.kdec2[bsel], ln.qdec2[bsel], ln.QKm2[bsel]
            Kb = lambda name: K(f"{name}{bsel}")
            col = j * 8 + h
            if j == jlo:
                B.memset("pool", Z[:, 0:128], 0.0, writes=[K("Z")])
                B.cp("pool", Z[:, 128:256], ident[:], reads=["ident"], writes=[K("Z")])
                B.cp("act", Zb, Z, reads=[K("Z")], writes=[K("Zb")])
                yield
            for cc in range(2):
                r0 = cc * 64
                pc = (j % 2) * 128 + r0
                egl = (eglA if cc == 0 else eglB)[:, col:col + 1]
                B.mm(P3[r0:r0 + 64, 0:256], wT[:, r0:r0 + 64], Zb, reads=[Kb("wT"), K("Zb")], writes=[k3])
                B.mm(P2[:, pc:pc + 64], Zb[:, 0:128], qdec[:, r0:r0 + 64], start=True, stop=False,
                     reads=[K("Zb"), Kb("qdec")], writes=[k2])
                yield
                B.tt("dve", vn[r0:r0 + 64, :], uext[r0:r0 + 64, :], P3[r0:r0 + 64, 0:256], ALU.subtract,
                     reads=[Kb("uext"), k3], writes=[K("vn")])
                yield
                B.mm(P2[:, pc:pc + 64], vn[r0:r0 + 64, 0:128], QKm[r0:r0 + 64, r0:r0 + 64], start=False, stop=True,
                     reads=[K("vn"), Kb("QKm")], writes=[k2])
                B.mm(P3[:, 256:512], kdec[r0:r0 + 64, :], vn[r0:r0 + 64, :], reads=[Kb("kdec"), K("vn")], writes=[k3])
                B.mm(P2[:, 256 + pc:256 + pc + 64], Zb[:, 128:256], qdec[:, r0:r0 + 64], start=True, stop=False,
                     reads=[K("Zb"), Kb("qdec")], writes=[k2])
                B.mm(P2[:, 256 + pc:256 + pc + 64], vn[r0:r0 + 64, 128:256], QKm[r0:r0 + 64, r0:r0 + 64], start=False, stop=True,
                     reads=[K("vn"), Kb("QKm")], writes=[k2])
                yield
                B.stt(Zb, Z, egl, P3[:, 256:512], ALU.mult, ALU.add, reads=[K("Z"), k3, "eglA", "eglB"], writes=[K("Zb")])
                B.stt(Z, Z, egl, P3[:, 256:512], ALU.mult, ALU.add, reads=[K("Z"), k3, "eglA", "eglB"], writes=[K("Z")])
                yield
            if j % 2 == 1:
                o0 = (j // 2) * 256
                B.cp("act", oT[:, h, o0:o0 + 256], P2[:, 0:256], reads=[k2], writes=[("oT", h, j // 4)])
                B.cp("dve", QtT[:, o0:o0 + 256], P2[:, 256:512], reads=[k2], writes=[("QtT", j // 4)])
                yield

        def fin_gen(h):
            lA, lB = lanes
            P6, P7 = psb[6], psb[7]
            k6, k7 = ("ps", 6), ("ps", 7)
            sqb = sq[:, 1, :]; rinv = tmpf[1][:]; szb = rstd[1][:]; otmp = tmpf[0][:]
            ksq, krinv, kszb, kotmp = ("sq", 1), ("tmpf", 1), ("rstd", 1), ("tmpf", 0)
            B.mm(P6[:, 0:128], lA.Zb[:, 128:256], ident_bf, reads=[lA.K("Zb"), "ident_bf"], writes=[k6])
            B.mm(P6[:, 128:256], lB.Zb[:, 128:256], ident_bf, reads=[lB.K("Zb"), "ident_bf"], writes=[k6])
            yield
            B.cp("act", PTA, P6[:, 0:128], reads=[k6], writes=["PTA"])
            B.cp("act", PTB, P6[:, 128:256], reads=[k6], writes=["PTB"])
            yield
            B.mm(P6[:, 256:384], PTB, lA.Zb[:, 0:128], reads=["PTB", lA.K("Zb")], writes=[k6])
            yield
            B.tt("dve", Sx, P6[:, 256:384], lB.Z[:, 0:128], ALU.add, reads=[k6, lB.K("Z")], writes=["Sx"])
            yield
            B.dma(sx_in[h * P:(h + 1) * P, :], Sx, reads=["Sx"], writes=[("sx_in", h)])
            if dbg.get("nocc"):
                B.dma(sx_out[h * 2 * P:h * 2 * P + P, :], sx_in[h * P:(h + 1) * P, :], reads=[("sx_in", h)], writes=[("sx_out", h)])
            else:
                s.add("pool", lambda e, h=h: e.collective_compute(
                    "AllGather", ALU.bypass, replica_groups=[[0, 1], [2, 3], [4, 5], [6, 7]],
                    ins=[sx_in[h * P:(h + 1) * P, :]], outs=[sx_out[h * 2 * P:(h + 1) * 2 * P, :]]),
                    [("sx_in", h)], [("sx_out", h)], dma=True, inc=CC_INC)
            B.dma(S0, sx_out[h * 2 * P:h * 2 * P + P, :], reads=[("sx_out", h)], writes=["Sx"])
            yield
            B.ts("dve", S0b, S0, hmask, None, ALU.mult, reads=["Sx", "hmask"], writes=["S0b"])
            yield
            B.mm(P6[:, 384:512], PTA, S0b, reads=["PTA", "S0b"], writes=[k6])
            yield
            B.tt("dve", Smidb, P6[:, 384:512], lA.Z[:, 0:128], ALU.add, reads=[k6, lA.K("Z")], writes=["Smidb"])
            yield

            def z_out(tt_i, ps, pk, c0, n):
                o0 = c0 - TH
                Sc, Sk = (S0b, "S0b") if tt_i <= 2 else (Smidb, "Smidb")
                B.act(szb, ps, AF.Silu, reads=[pk], writes=[kszb])
                B.mm(P6[:, 0:n], Sc, QtT[:, o0:o0 + n], reads=[Sk, ("QtT", tt_i - 1)], writes=[k6])
                yield
                B.tt("dve", otmp, P6[:, 0:n], oT[:, h, o0:o0 + n], ALU.add, reads=[k6, ("oT", h, tt_i - 1)], writes=[kotmp])
                yield
                B.act(sqb, otmp, AF.Square, reads=[kotmp], writes=[ksq])
                B.mm(P7[:, 0:n], ones_bf[:], sqb, reads=[ksq, "ones"], writes=[k7])
                yield
                B.act(rinv, P7[:, 0:n], AF.Ln, bias=eps_ap, scale=1.0 / 128, reads=[k7, "eps"], writes=[krinv])
                B.act(rinv, rinv, AF.Exp, scale=-0.5, reads=[krinv], writes=[krinv])
                yield
                B.stt(otmp, otmp, dng, rinv, ALU.mult, ALU.mult, reads=[kotmp, "dng", krinv], writes=[kotmp])
                B.tt("dve", oT[:, h, o0:o0 + n], otmp, szb, ALU.mult, reads=[kotmp, kszb], writes=[("oT", h, tt_i - 1)])
                yield
            yield from proj_gen(3584 + h * 128, range(1, 5), z_out, [4, 5])

        run(h1_gen(0))
        for h in range(8):
            lA_, lB_ = lanes
            run(prep_tile(h, lA_, 0, 0, 8), prep_tile(h, lB_, 8, 8, 16))
            for t in range(8):
                gens = []
                if t + 1 < 8:
                    gens += [prep_tile(h, lA_, t + 1, 0, 8), prep_tile(h, lB_, 8 + t + 1, 8, 16)]
                gens += [recur_tile(h, lA_, t, 0, 8), recur_tile(h, lB_, 8 + t, 8, 16)]
                run(*gens)
            if h < 7:
                run(h1_gen(h + 1), fin_gen(h))
            else:
                run(fin_gen(h))

        c.off = mark
        s.barrier()
        xpT = c.bf16(4 * TE).rearrange("p (g t) -> p g t", g=4)
        mark2 = c.off
        for g in range(4):
            run(proj_gen(g * 128, range(len(TT)), evac_masked(lambda c0, n, g=g: xpT[:, g, c0:c0 + n], ("xpT", g)), [0, 1]))
        poolw_st = c.f32(4 * 128).rearrange("p (g c) -> p g c", g=4)
        poolw_bf = c.bf16(4 * 128).rearrange("p (g c) -> p g c", g=4)
        for g in range(4):
            B.dma(poolw_st[:, g, :], poolw_d[g], writes=["poolw_st"])
        B.cp("pool", poolw_bf, poolw_st, reads=["poolw_st"], writes=["poolw_bf"])
        sA = c.f32(528); sB = c.f32(528); pooled = [c.bf16(512) for i in range(2)]
        WINS = (2, 4, 8, 16)
        for tt_i in (4, 3, 2, 1):
            c0, n = TT[tt_i]
            for g in range(4):
                w = WINS[g]
                src = xpT[:, g, c0 - 16:c0 + n]
                cur, ckey = src, ("xpT", g)
                lo = 0
                bufs = [(sA, "sA"), (sB, "sB")]
                bi = 0
                sh = 1
                while sh < w:
                    dst, dkey = bufs[bi]
                    bi ^= 1
                    nlo = lo + sh
                    B.tt("pool" if g % 2 else "dve", dst[:, nlo:528], cur[:, nlo:528], cur[:, nlo - sh:528 - sh], ALU.add,
                         reads=[ckey], writes=[dkey])
                    cur, ckey, lo = dst, dkey, nlo
                    sh *= 2
                pb = pooled[g % 2]
                pk = ("pooled", g % 2)
                s.add("dve", lambda e, pb=pb, cur=cur, src=src, w=w, n=n: e.scalar_tensor_tensor(
                    pb[:, 0:n], cur[:, 16:16 + n], 1.0 / w, src[:, 16:16 + n], ALU.mult, ALU.subtract),
                    [ckey, ("xpT", g)], [pk])
                if tt_i == 1:
                    oth, okey = (sB, "sB") if cur is sA else (sA, "sA")
                    B.tt("dve", oth[:, 0:16], cur[:, 16:32], invdiv[:, g * 16:(g + 1) * 16], ALU.mult,
                         reads=[ckey, "invdiv"], writes=[okey])
                    B.tt("dve", pb[:, 0:16], oth[:, 0:16], src[:, 16:32], ALU.subtract,
                         reads=[okey, ("xpT", g)], writes=[pk])
                B.mm(psb[2 + g % 2][:, 0:n], poolw_bf[:, g, :], pb[:, 0:n], reads=["poolw_bf", pk], writes=[("ps", 2 + g % 2)])
                B.act(xpT[:, g, c0:c0 + n], psb[2 + g % 2][:, 0:n], AF.Copy, scale=pscale[:, g:g + 1],
                      reads=[("ps", 2 + g % 2), "pscale", "sA", "sB"], writes=[("xpT", g)])
        yaT = xpT
        c.off = mark2
        s.barrier()
        dnw = [c.bf16(8 * 128).rearrange("p (g c) -> p g c", g=8) for i in range(2)]
        ppw = [c.bf16(4 * 128).rearrange("p (g c) -> p g c", g=4) for i in range(2)]
        mow = [c.bf16(D) for i in range(2)]
        wgpb = [c.bf16(8 * 128).rearrange("p (g c) -> p g c", g=8) for i in range(2)]
        wgdb = [c.bf16(8 * 128).rearrange("p (g c) -> p g c", g=8) for i in range(2)]
        tsts = [c.f32(KC * 128) for i in range(2)]
        mrg = [c.bf16(512) for i in range(2)]
        sg1 = rstd[0][:]; sg2 = rstd[1][:]; m1 = tmpf[0][:]
        tctr = [0]

        def stage(dst, srcs, rows3=True):
            i = tctr[0] % 2
            tctr[0] += 1
            tst = tsts[i]
            tst3 = tst.rearrange("p (k c) -> p k c", k=KC)
            if rows3:
                for k, src in enumerate(srcs):
                    B.dma(tst3[:, k, :], src, writes=[("tst", i, k)])
                B.cp("pool", dst[0], tst3[:, 0:len(srcs), :], reads=[("tst", i, k) for k in range(len(srcs))],
                     writes=[dst[1]] + [("tst", i, k) for k in range(KC)])
            else:
                B.dma(tst, srcs, writes=[("tst", i, k) for k in range(KC)])
                B.cp("pool", dst[0], tst, reads=[("tst", i, k) for k in range(KC)],
                     writes=[dst[1]] + [("tst", i, k) for k in range(KC)])

        def load_tail(e_):
            sl = e_ % 2
            esl = slice(e_ * 128, (e_ + 1) * 128)
            stage((ppw[sl], ("ppw", sl)), [poolproj_d[k * 128:(k + 1) * 128, esl] for k in range(4)])
            stage((wgpb[sl], ("wgp", sl)), [mix_w[k * 128:(k + 1) * 128, 4624 + e_ * 128:4624 + (e_ + 1) * 128] for k in range(KC)])
            stage((dnw[sl], ("dnw", sl)), [dnproj_d[k * 128:(k + 1) * 128, esl] for k in range(KC)])
            stage((wgdb[sl], ("wgd", sl)), [mix_w[k * 128:(k + 1) * 128, 5648 + e_ * 128:5648 + (e_ + 1) * 128] for k in range(KC)])
            stage((mow[sl], ("mow", sl)), mixout_d[esl, :], rows3=False)

        def projpart(e_, tt_i):
            sl = e_ % 2
            c0, n = TT[tt_i]
            o0 = c0 - TH
            for g in range(4):
                B.mm(psb[0][:, 0:n], ppw[sl][:, g, :], yaT[:, g, c0:c0 + n], start=(g == 0), stop=(g == 3),
                     reads=[("ppw", sl)] + [("xpT", gg) for gg in range(4)], writes=[("ps", 0)])
            for k in range(KC):
                B.mm(psb[1][:, 0:n], wgpb[sl][:, k, :], hT[:, k, c0:c0 + n], start=(k == 0), stop=(k == KC - 1),
                     reads=[("wgp", sl), ("hT", tt_i)], writes=[("ps", 1)])
            for hh in range(8):
                B.mm(psb[2][:, 0:n], dnw[sl][:, hh, :], oT[:, hh, o0:o0 + n], start=(hh == 0), stop=(hh == 7),
                     reads=[("dnw", sl)] + [("oT", hh, tt_i - 1) for hh in range(8)], writes=[("ps", 2)])
            for k in range(KC):
                B.mm(psb[3][:, 0:n], wgdb[sl][:, k, :], hT[:, k, c0:c0 + n], start=(k == 0), stop=(k == KC - 1),
                     reads=[("wgd", sl), ("hT", tt_i)], writes=[("ps", 3)])
            B.act(sg1, psb[1][:, 0:n], AF.Sigmoid, reads=[("ps", 1)], writes=[("rstd", 0)])
            B.act(sg2, psb[3][:, 0:n], AF.Sigmoid, reads=[("ps", 3)], writes=[("rstd", 1)])
            B.tt("dve", m1, sg1, psb[0][:, 0:n], ALU.mult, reads=[("rstd", 0), ("ps", 0)], writes=[("tmpf", 0)])
            B.tt("dve", sg2, sg2, psb[2][:, 0:n], ALU.mult, reads=[("rstd", 1), ("ps", 2)], writes=[("rstd", 1)])
            mb = mrg[tt_i % 2]
            B.tt("dve", mb[:, 0:n], sg2, m1, ALU.add, reads=[("rstd", 1), ("tmpf", 0)], writes=[("mrg", tt_i % 2)])

        def mowpart(e_, tt_i):
            sl = e_ % 2
            c0, n = TT[tt_i]
            xk = xkeys(c0, n)
            mb = mrg[tt_i % 2]
            for d in range(KC):
                bi = 4 + d % 4
                B.mm(psb[bi][:, 0:n], mow[sl][:, d * 128:(d + 1) * 128], mb[:, 0:n],
                     reads=[("mow", sl), ("mrg", tt_i % 2)], writes=[("ps", bi)])
                B.stt(xT[:, d, c0:c0 + n], psb[bi][:, 0:n], hgT[:, 8 + d:8 + d + 1], xT[:, d, c0:c0 + n],
                      ALU.mult, ALU.add, reads=[("ps", bi), ("hgT", 1)] + xk, writes=xk)

        load_tail(0)
        prev = None
        for e_ in range(KC):
            for tt_i in range(1, 5):
                projpart(e_, tt_i)
                if prev is not None:
                    mowpart(*prev)
                if tt_i == 1 and e_ + 1 < KC:
                    load_tail(e_ + 1)
                prev = (e_, tt_i)
        mowpart(*prev)

    if dbg.get("ffn1", True):
        per = (len(ada_rest) + len(GROUPS) - 1) // len(GROUPS)
        ffn(0, w1_in, w1_out, [0, 1, 2, 3, 4], hook=lambda gi: ada_rest[gi * per:(gi + 1) * per])
    else:
        for blk in ada_rest:
            ada_dma(blk)
            ada_mm(blk)
    ada_finish([1, 2])
    if dbg.get("mixer", True):
        s.barrier()
        mixer()
        s.barrier()
    if dbg.get("ffn2", True):
        ffn(2, w2_in, w2_out, [1, 2, 3, 4])
    s.barrier()

    outb = [scr[:, 4096 + i * 1024:4096 + (i + 1) * 1024] for i in range(2)]
    yT = scr[:, 0:4096].rearrange("p (k c) -> p k c", k=KC)
    for tt_i in range(1, len(TT)):
        def out_fn(k, t_ap, tkey, c0, n):
            B.cp("act", yT[:, k, 0:n], t_ap, reads=[tkey], writes=[("yT", k), ("scr", 0), ("scr", 1), ("scr", 2)])
        norm_tile(tt_i, lambda k: norm_gT[:, 24 + k:24 + k + 1], ["norm_gT"], out_fn)
        c0, n = TT[tt_i]
        for sub in range(4):
            ob_i = (tt_i * 4 + sub) % 2
            ob = outb[ob_i]
            for half in range(2):
                bank = psb[half]
                bkey = ("ps", half)
                for j in range(4):
                    k = half * 4 + j
                    B.tr(bank[:, j * 128:(j + 1) * 128], yT[:, k, sub * 128:(sub + 1) * 128], ident[:],
                         reads=[("yT", k), "ident"], writes=[bkey])
                B.cp("act" if half == 0 else "dve", ob[:, half * 512:(half + 1) * 512], bank[:],
                     reads=[bkey], writes=[("outb", ob_i), ("scr", 3)])
            r0 = (tt_i - 1) * 512 + sub * 128
            B.dma(out_d[r0:r0 + 128, :], ob[:], reads=[("outb", ob_i)], writes=[("outd", r0)])

    return finish(B)


_CACHE = {}


def _layout_T(v, nchunk):
    return np.ascontiguousarray(v.reshape(nchunk, P).T)


def kernel(**inp):
    x = np.asarray(inp["x"], np.float32)
    c = np.asarray(inp["c"], np.float32)
    if "B" not in _CACHE:
        _CACHE["B"] = build()
    B = _CACHE["B"]
    ada_bT = _layout_T(np.asarray(inp["ada_b"], np.float32)[0], 72)
    ng = np.asarray(inp["norm_g"], np.float32)[0]
    norm_gT = np.concatenate([_layout_T(ng[i], 8) for i in range(3)] +
                             [_layout_T(np.asarray(inp["final_g"], np.float32), 8)], axis=1)
    ident = np.eye(P, dtype=np.float32)
    shared = {
        "ada_w": np.ascontiguousarray(np.asarray(inp["ada_w"], np.float32)[0]),
        "ada_bT": np.ascontiguousarray(ada_bT),
        "norm_gT": np.ascontiguousarray(norm_gT),
        "ident": ident,
        "ffn1_w_in": np.ascontiguousarray(np.asarray(inp["ffn1_w_in"], np.float32)[0]),
        "ffn1_w_out": np.ascontiguousarray(np.asarray(inp["ffn1_w_out"], np.float32)[0]),
        "ffn2_w_in": np.ascontiguousarray(np.asarray(inp["ffn2_w_in"], np.float32)[0]),
        "ffn2_w_out": np.ascontiguousarray(np.asarray(inp["ffn2_w_out"], np.float32)[0]),
    }
    f32 = lambda k: np.asarray(inp[k], np.float32)
    cw = f32("conv_w")[0]
    conv_wT = np.ascontiguousarray(cw.reshape(4, 24, P).transpose(2, 1, 0).reshape(P, 96))
    idx = np.arange(P)
    blk64 = (idx[:, None] // 64) == (idx[None, :] // 64)
    blk16 = (idx[:, None] // 16) == (idx[None, :] // 16)
    mLs = blk64 & (idx[:, None] > idx[None, :])
    mUi = blk64 & (idx[None, :] >= idx[:, None])
    Lc = blk64 & (idx[:, None] <= idx[None, :])
    LA = np.broadcast_to((idx[:, None] < 64), (P, P))
    LB = np.broadcast_to((idx[:, None] >= 64), (P, P))
    cst = np.concatenate([m.astype(np.float32) for m in
                          (np.where(mLs, 0.0, 1e5), np.where(mUi, 0.0, -1e5), blk16, ~blk16, Lc, blk64, LA, LB,
                           np.ones((P, P), bool))], axis=1)
    shared.update({
        "mix_w_in": np.ascontiguousarray(f32("mix_w_in")[0]),
        "conv_wT": conv_wT,
        "alog_rep": np.ascontiguousarray(np.tile(f32("a_log")[0][None, None, :], (P, 16, 1)).reshape(P, 128)),
        "dtb_rep": np.ascontiguousarray(np.tile(f32("dt_bias")[0][None, None, :], (P, 16, 1)).reshape(P, 128)),
        "dng": np.ascontiguousarray(f32("dn_norm_g")[0].reshape(P, 1)),
        "pool_w": np.ascontiguousarray(f32("pool_w")[0]),
        "pool_scaleT": _layout_T(f32("pool_scale")[0], 4),
        "pool_proj": np.ascontiguousarray(f32("pool_proj")[0]),
        "dn_proj": np.ascontiguousarray(f32("dn_proj")[0]),
        "mix_w_out": np.ascontiguousarray(f32("mix_w_out")[0]),
        "cst": np.ascontiguousarray(cst),
    })
    WINS = (2, 4, 8, 16)
    invdiv0 = np.stack([1.0 / np.minimum(np.arange(1, 17), w) for w in WINS]).astype(np.float32).reshape(1, 64)
    invdiv1 = np.stack([np.full(16, 1.0 / w) for w in WINS]).astype(np.float32).reshape(1, 64)
    in_maps = []
    for i in range(NCORES):
        b, sh = i // 2, i % 2
        xe = np.zeros((TE, D), np.float32)
        xe[TH:] = x[b, sh * T:(sh + 1) * T]
        if sh == 1:
            xe[:TH] = x[b, T - TH:T]
        m = dict(shared)
        m["x_ext"] = xe
        m["cT"] = _layout_T(c[b], 8)
        m["halo_mask"] = np.full((P, 1), float(sh), np.float32)
        m["invdiv"] = np.ascontiguousarray(np.tile(invdiv1 if sh else invdiv0, (P, 1)))
        in_maps.append(m)
    ncr = B.debug.get("ncores", NCORES)
    res = run_bass_kernel_spmd(B.nc, in_maps[:ncr], core_ids=list(range(ncr)))
    out = np.zeros((4, 2 * T, D), np.float32)
    for i in range(ncr):
        b, sh = i // 2, i % 2
        out[b, sh * T:(sh + 1) * T] = res.results[i]["out"]
    return out
```

```python
import numpy as np
import ml_dtypes
import concourse.bass as bass
import concourse.mybir as mybir
from concourse.bass_utils import run_bass_kernel_spmd

F32 = mybir.dt.float32
BF16 = mybir.dt.bfloat16
AF = mybir.ActivationFunctionType
ALU = mybir.AluOpType

P = 128
D = 1024
KC = 8
T = 2048
TH = 16
TE = T + TH
FH = 2816
NF = 22
MIXW = 6672
TT = [(0, TH)] + [(TH + 512 * i, 512) for i in range(4)]
NCORES = 8
CC_INC = 1


class Op:
    __slots__ = ("eng", "fn", "deps", "sig", "cnt", "dma", "dsem", "dval", "prev_dma", "idx", "dinc")


class Sched:
    ENGS = ("sp", "pe", "act", "dve", "pool")

    def __init__(self, nds=40):
        self.ops = []
        self.last_w = {}
        self.readers = {}
        self.nds = nds
        self.ndma = 0
        self.dma_ops = []
        self.bar_deps = set()
        self.bar_seen = set(self.ENGS)
        self.last_eng = {}
        self.dma_since_bar = []
        self.dtot = {}

    def barrier(self):
        self.bar_deps = set(self.last_eng.values()) | set(self.dma_since_bar)
        self.bar_seen = set()
        self.dma_since_bar = []

    @staticmethod
    def _expand(keys, is_write):
        out = []
        for k in keys:
            if isinstance(k, tuple) and len(k) == 2 and isinstance(k[0], str) and k[0].startswith("ps") \
                    and k[0] != "ps" and k[0][2:].isdigit():
                k = ("ps", int(k[0][2:]))
            if k not in out:
                out.append(k)
        return out

    def add(self, eng, fn, reads=(), writes=(), dma=False, inc=16):
        reads = self._expand(list(reads), False)
        writes = self._expand(list(writes), True)
        psr = [k for k in reads if isinstance(k, tuple) and k[0] == "ps"]
        if psr:
            reads = [k for k in reads if k not in psr]
            writes = writes + [k for k in psr if k not in writes]
        op = Op()
        op.eng = eng; op.fn = fn; op.sig = False; op.cnt = 0; op.dma = dma
        op.idx = len(self.ops)
        op.prev_dma = None
        deps = set()
        for r in reads:
            w = self.last_w.get(r)
            if w is not None:
                deps.add(w)
        for w_ in writes:
            w = self.last_w.get(w_)
            if w is not None:
                deps.add(w)
            for rd in self.readers.get(w_, ()):
                deps.add(rd)
        if eng not in self.bar_seen:
            deps |= self.bar_deps
            self.bar_seen.add(eng)
        deps.discard(op)
        op.deps = deps
        if dma:
            self.dma_since_bar.append(op)
        else:
            self.last_eng[eng] = op
        for r in reads:
            rs = self.readers.setdefault(r, set())
            if not dma:
                for o in [o for o in rs if (not o.dma) and o.eng == eng]:
                    rs.discard(o)
            rs.add(op)
        for w_ in writes:
            self.last_w[w_] = op
            self.readers[w_] = set()
        if dma:
            i = self.ndma
            self.ndma += 1
            op.dsem = i % self.nds
            self.dtot[op.dsem] = self.dtot.get(op.dsem, 0) + inc
            op.dval = self.dtot[op.dsem]
            op.dinc = inc
            if i >= self.nds:
                op.prev_dma = self.dma_ops[i - self.nds]
            self.dma_ops.append(op)
        self.ops.append(op)
        return op

    def finalize(self):
        for op in self.ops:
            for d in op.deps:
                if d.dma:
                    continue
                if d.eng == "pe" and op.eng == "pe" and not op.dma:
                    continue
                d.sig = True
        cnt = {e: 0 for e in self.ENGS}
        for op in self.ops:
            if op.dma:
                continue
            if op.sig:
                cnt[op.eng] += 1
                op.cnt = cnt[op.eng]
        return cnt

    def emit(self, eng, e, sems, dsems, final_wait_all_dma=False):
        seen = {}
        n_wait = 0
        for op in self.ops:
            if op.eng != eng:
                continue
            waits = []
            for d in op.deps:
                if d.dma:
                    waits.append((("d", d.dsem), d.dval))
                else:
                    if d.eng == "pe" and eng == "pe" and not op.dma:
                        continue
                    waits.append((("e", d.eng), d.cnt))
            if op.dma and op.prev_dma is not None:
                waits.append((("d", op.prev_dma.dsem), op.prev_dma.dval))
            best = {}
            for k, v in waits:
                if v > best.get(k, 0):
                    best[k] = v
            for k, v in best.items():
                if seen.get(k, 0) >= v:
                    continue
                seen[k] = v
                s = dsems[k[1]] if k[0] == "d" else sems[k[1]]
                e.wait_ge(s, v)
                n_wait += 1
            ins = op.fn(e)
            if op.dma:
                ins.then_inc(dsems[op.dsem], op.dinc)
            elif op.sig:
                ins.then_inc(sems[eng], 1)
        if final_wait_all_dma:
            last = {}
            for op in self.dma_ops:
                last[op.dsem] = max(last.get(op.dsem, 0), op.dval)
            for k, v in last.items():
                e.wait_ge(dsems[k], v)
        return n_wait


class Builder:
    def __init__(self, debug=None):
        self.nc = bass.Bass("TRN2", target_bir_lowering=False)
        self.s = Sched()
        self.debug = debug or {}
        self.dram = {}
        self.nps = 0

    def din(self, name, shape, dt=F32):
        t = self.nc.dram_tensor(name, list(shape), dt, kind="ExternalInput")
        self.dram[name] = t
        return t.ap()

    def dout(self, name, shape, dt=F32):
        t = self.nc.dram_tensor(name, list(shape), dt, kind="ExternalOutput")
        self.dram[name] = t
        return t.ap()

    def sb(self, name, shape, dt=F32):
        return self.nc.alloc_sbuf_tensor(name, list(shape), dt)

    def ps(self, name, shape=(P, 512), dt=F32):
        return self.nc.alloc_psum_tensor(name, list(shape), dt)

    def dma(self, out, in_, reads=(), writes=(), q="sp", **kw):
        return self.s.add(q, lambda e: e.dma_start(out=out, in_=in_, **kw), reads, writes, dma=True)

    def mm(self, out, lhsT, rhs, start=True, stop=True, reads=(), writes=()):
        return self.s.add("pe", lambda e: e.matmul(out, lhsT, rhs, start=start, stop=stop), reads, writes)

    def tr(self, out, in_, ident, reads=(), writes=()):
        return self.s.add("pe", lambda e: e.transpose(out, in_, ident), reads, writes)

    def act(self, out, in_, func, bias=None, scale=None, reads=(), writes=(), accum_out=None):
        kw = {}
        if bias is not None:
            kw["bias"] = bias
        if scale is not None:
            kw["scale"] = scale
        if accum_out is not None:
            kw["accum_out"] = accum_out
        return self.s.add("act", lambda e: e.activation(out, in_, func, **kw), reads, writes)

    def tt(self, eng, out, in0, in1, op, reads=(), writes=()):
        return self.s.add(eng, lambda e: e.tensor_tensor(out, in0, in1, op), reads, writes)

    def ts(self, eng, out, in0, s1, s2, op0, op1=None, reads=(), writes=()):
        if op1 is None:
            return self.s.add(eng, lambda e: e.tensor_scalar(out, in0, s1, None, op0), reads, writes)
        return self.s.add(eng, lambda e: e.tensor_scalar(out, in0, s1, s2, op0, op1), reads, writes)

    def stt(self, out, in0, scalar, in1, op0, op1, reads=(), writes=()):
        return self.s.add("dve", lambda e: e.scalar_tensor_tensor(out, in0, scalar, in1, op0, op1), reads, writes)

    def cp(self, eng, out, in_, reads=(), writes=()):
        if eng == "act":
            return self.s.add("act", lambda e: e.activation(out, in_, AF.Copy), reads, writes)
        return self.s.add(eng, lambda e: e.tensor_copy(out, in_), reads, writes)

    def memset(self, eng, ap, val, writes=()):
        return self.s.add(eng, lambda e: e.memset(ap, val), (), writes)


def finish(B):
    nc = B.nc
    s = B.s
    cnt = s.finalize()
    engmap = {"sp": "sync", "pe": "tensor", "act": "scalar", "dve": "vector", "pool": "gpsimd"}
    from contextlib import ExitStack
    with ExitStack() as es:
        sems = {e: es.enter_context(nc.semaphore(f"sem_{e}")) for e in Sched.ENGS}
        dsems = [es.enter_context(nc.semaphore(f"dsem{i}")) for i in range(s.nds)]
        block = es.enter_context(nc.Block())
        stats = {}

        @block.sync
        def _(e):
            stats["sp"] = s.emit("sp", e, sems, dsems, final_wait_all_dma=True)

        @block.tensor
        def _(e):
            stats["pe"] = s.emit("pe", e, sems, dsems)

        @block.scalar
        def _(e):
            stats["act"] = s.emit("act", e, sems, dsems)

        @block.vector
        def _(e):
            stats["dve"] = s.emit("dve", e, sems, dsems)

        @block.gpsimd
        def _(e):
            stats["pool"] = s.emit("pool", e, sems, dsems)
    B.stats = (cnt, stats, len(s.ops))
    return B


def build(debug=None):
    B = Builder(debug)
    nc = B.nc
    s = B.s
    dbg = B.debug

    x_ext = B.din("x_ext", [TE, D])
    cT_d = B.din("cT", [P, KC])
    ada_w = B.din("ada_w", [D, 9 * D])
    ada_bT_d = B.din("ada_bT", [P, 72])
    norm_gT_d = B.din("norm_gT", [P, 32])
    ident_d = B.din("ident", [P, P])
    w1_in = B.din("ffn1_w_in", [D, 2 * FH])
    w1_out = B.din("ffn1_w_out", [FH, D])
    w2_in = B.din("ffn2_w_in", [D, 2 * FH])
    w2_out = B.din("ffn2_w_out", [FH, D])
    out_d = B.dout("out", [T, D])

    xT = B.sb("xT", [P, KC, TE], F32)
    hT = B.sb("hT", [P, KC, TE], BF16)
    ident = B.sb("ident_sb", [P, P], F32)
    ones_bf = B.sb("ones_bf", [P, P], BF16)
    cT = B.sb("cTs", [P, KC], F32)
    scT = B.sb("scT", [P, KC], F32)
    ada_bT = B.sb("ada_bTs", [P, 72], F32)
    modT = B.sb("modT", [P, 72], F32)
    norm_gT = B.sb("norm_gTs", [P, 32], F32)
    gsT = B.sb("gsT", [P, 24], F32)
    hgT = B.sb("hgT", [P, 24], F32)
    small = B.sb("small", [P, 2048], F32)
    NREG = (nc.sbuf_bytes_remaining - 2048 - 4096 - 4096 - 512) // 4 // 8 * 8
    REG = B.sb("reg", [P, NREG], F32)
    scr = REG[:, 0:6144]

    class Carver:
        def __init__(self, off=0):
            self.off = off

        def f32(self, n):
            ap = REG[:, self.off:self.off + n]
            self.off += n
            assert self.off <= NREG, (self.off, NREG)
            return ap

        def bf16(self, n):
            m = (n + 1) // 2
            ap = REG[:, self.off:self.off + m].bitcast(BF16)[:, 0:n]
            self.off += m
            assert self.off <= NREG, (self.off, NREG)
            return ap
    xin = [scr[:, i * 1024:(i + 1) * 1024] for i in range(2)]
    sq = B.sb("sq", [P, 2, 512], BF16)
    rstd = [B.sb(f"rstd{i}", [P, 512], F32) for i in range(2)]
    tmpf = [B.sb(f"tmpf{i}", [P, 512], F32) for i in range(2)]

    psb = [B.ps(f"psb{i}") for i in range(8)]

    B.dma(ident[:], ident_d, writes=["ident"])
    B.memset("pool", ones_bf[:], 1.0, writes=["ones"])
    B.dma(cT[:], cT_d, writes=["cT"])
    B.dma(ada_bT[:], ada_bT_d, writes=["ada_bT"])
    B.dma(norm_gT[:], norm_gT_d, writes=["norm_gT"])

    rowtiles = [(0, TH)] + [(TH + 128 * i, 128) for i in range(16)]
    for ti, (r0, n) in enumerate(rowtiles):
        xb = xin[ti % 2]
        B.dma(xb[0:n, :], x_ext[r0:r0 + n, :], writes=[("xin", ti % 2), ("scr", ti % 2)])
        for half in range(2):
            bank = psb[(2 * ti + half) % 4]
            bkey = ("ps", (2 * ti + half) % 4)
            for j in range(4):
                k = half * 4 + j
                B.tr(bank[:, j * 128:j * 128 + n], xb[0:n, k * 128:(k + 1) * 128], ident[0:n, 0:n],
                     reads=[("xin", ti % 2), "ident"], writes=[bkey])
            src = bank[:].rearrange("p (j c) -> p j c", j=4)[:, :, 0:n]
            B.cp("act" if half == 0 else "dve", xT[:, half * 4:half * 4 + 4, r0:r0 + n], src,
                 reads=[bkey], writes=[("xT", ti)])
    XT_ALL = [("xT", ti) for ti in range(len(rowtiles))]

    def xkeys(c0, n):
        ks = []
        for ti, (r0, m) in enumerate(rowtiles):
            if r0 < c0 + n and c0 < r0 + m:
                ks.append(("xT", ti))
        return ks

    B.act(scT[:], cT[:], AF.Silu, reads=["cT"], writes=["scT"])
    AW = 256
    adaw_st = [scr[:, i * 2048:(i + 1) * 2048].rearrange("p (k c) -> p k c", k=KC) for i in range(3)]
    adaw_keys = [[("scr", 0), ("scr", 1)], [("scr", 2)], [("scr", 3)]]
    nblk = 9 * D // AW
    NB0 = 24 * 128 // AW

    def ada_dma(blk):
        st = adaw_st[(blk + 1) % 3]
        for k in range(KC):
            B.dma(st[:, k, :], ada_w[k * 128:(k + 1) * 128, blk * AW:(blk + 1) * AW],
                  writes=[("adaw", (blk + 1) % 3, k)])

    def ada_mm(blk):
        st = adaw_st[(blk + 1) % 3]
        for jj in range(AW // 128):
            j = blk * (AW // 128) + jj
            for k in range(KC):
                B.mm(psb[7][:, j:j + 1], st[:, k, jj * 128:(jj + 1) * 128], scT[:, k:k + 1],
                     start=(k == 0), stop=(k == KC - 1),
                     reads=[("adaw", (blk + 1) % 3, k), "scT"], writes=[("ps", 7)])

    def ada_finish(sls):
        for sl in sls:
            cs = slice(sl * 24, (sl + 1) * 24)
            B.tt("dve", modT[:, cs], psb[7][:, cs], ada_bT[:, cs], ALU.add, reads=[("ps", 7), "ada_bT"], writes=["modT"])
            sc = modT[:, (sl * 3 + 1) * 8:(sl * 3 + 2) * 8]
            gt = modT[:, (sl * 3 + 2) * 8:(sl * 3 + 3) * 8]
            s.add("dve", lambda e, sc=sc, sl=sl: e.scalar_tensor_tensor(
                gsT[:, sl * 8:(sl + 1) * 8], sc, 1.0, norm_gT[:, sl * 8:(sl + 1) * 8], ALU.add, ALU.mult),
                ["modT", "norm_gT"], [("gsT", sl)])
            B.ts("dve", hgT[:, sl * 8:(sl + 1) * 8], gt, 0.5 if sl != 1 else 1.0, None, ALU.mult,
                 reads=["modT"], writes=[("hgT", sl)])

    B.memset("pool", adaw_st[0][0:1, 0, 0:1], 0.0,
             writes=[("adaw", 0, k) for k in range(KC)] + [("xin", 0), ("xin", 1)])
    for blk in range(NB0):
        ada_dma(blk)
        ada_mm(blk)
    ada_finish([0])
    ada_rest = list(range(NB0, nblk))

    def norm_tile(tt_i, gs_ap_fn, gs_keys, out_fn, shift_fn=None):
        c0, n = TT[tt_i]
        xk = xkeys(c0, n)
        rs = rstd[tt_i % 2]
        rkey = ("rstd", tt_i % 2)
        for k in range(KC):
            B.act(sq[:, k % 2, 0:n], xT[:, k, c0:c0 + n], AF.Square, reads=xk, writes=[("sq", k % 2)])
            B.mm(psb[6][:, 0:n], ones_bf[:], sq[:, k % 2, 0:n], start=(k == 0), stop=(k == KC - 1),
                 reads=[("sq", k % 2), "ones"], writes=[("ps", 6)])
        B.act(rs[:, 0:n], psb[6][:, 0:n], AF.Ln, bias=eps_ap, scale=1.0 / D, reads=[("ps", 6), "eps"], writes=[rkey])
        B.act(rs[:, 0:n], rs[:, 0:n], AF.Exp, scale=-0.5, reads=[rkey], writes=[rkey])
        for k in range(KC):
            tb = tmpf[k % 2]
            tkey = ("tmpf", k % 2)
            B.stt(tb[:, 0:n], xT[:, k, c0:c0 + n], gs_ap_fn(k), rs[:, 0:n], ALU.mult, ALU.mult,
                  reads=xk + [rkey] + gs_keys, writes=[tkey])
            out_fn(k, tb[:, 0:n], tkey, c0, n)

    eps_t = B.sb("eps_t", [P, 1], F32)
    B.memset("pool", eps_t[:], 1e-6, writes=["eps"])
    eps_ap = eps_t[:, 0:1]

    GROUPS = [3, 3, 3, 3, 3, 3, 2, 2]
    FG = 3
    cf = Carver(6144)
    win_st = [cf.f32(2 * FG * 128).rearrange("p (a c) -> p a c", a=2) for i in range(2)]
    wout_st = [cf.f32(D) for i in range(2)]
    win_bf = [cf.bf16(KC * 2 * FG * 128).rearrange("p (k a c) -> p k a c", k=KC, a=2) for i in range(2)]
    wout_bf = [cf.bf16(FG * D).rearrange("p (f c) -> p f c", f=FG) for i in range(2)]
    actb = [cf.bf16(FG * 512).rearrange("p (f c) -> p f c", f=FG) for i in range(2)]
    sgb = [cf.f32(512) for i in range(2)]
    st_ctr = [0, 0]

    def ffn(sl, w_in, w_out, tiles, hook=None):
        shift = lambda k: modT[:, (sl * 3) * 8 + k:(sl * 3) * 8 + k + 1]
        for tt_i in tiles:
            def out_fn(k, t_ap, tkey, c0, n):
                B.act(hT[:, k, c0:c0 + n], t_ap, AF.Identity, bias=shift(k), scale=1.0,
                      reads=[tkey, "modT"], writes=[("hT", tt_i)])
            norm_tile(tt_i, lambda k: gsT[:, sl * 8 + k:sl * 8 + k + 1], [("gsT", sl)], out_fn)
        f0s = [sum(GROUPS[:i]) for i in range(len(GROUPS))]

        def load_group(gi):
            fg = GROUPS[gi]
            f0 = f0s[gi]
            slot = gi % 2
            wb = win_bf[slot]
            ob = wout_bf[slot]
            ncol = fg * 128
            for k in range(KC):
                si = st_ctr[0] % 2
                st_ctr[0] += 1
                stg = win_st[si]
                skey = ("win_st", si)
                for gu in range(2):
                    B.dma(stg[:, gu, 0:ncol],
                          w_in[k * 128:(k + 1) * 128, gu * FH + f0 * 128:gu * FH + f0 * 128 + ncol],
                          writes=[skey])
                B.cp("pool", wb[:, k, :, 0:ncol], stg[:, :, 0:ncol],
                     reads=[skey], writes=[("win_bf", slot)])
            for ff in range(fg):
                si = st_ctr[1] % 2
                st_ctr[1] += 1
                f = f0 + ff
                B.dma(wout_st[si], w_out[f * 128:(f + 1) * 128, :], writes=[("wout_st", si)])
                B.cp("pool", ob[:, ff, :], wout_st[si],
                     reads=[("wout_st", si)], writes=[("wout_bf", slot)])

        load_group(0)
        for gi, fg in enumerate(GROUPS):
            hk = hook(gi) if hook else []
            for blk in hk:
                ada_dma(blk)
            if gi + 1 < len(GROUPS):
                load_group(gi + 1)
            slot = gi % 2
            wb = win_bf[slot]
            ob = wout_bf[slot]

            def p1(tt_i):
                c0, n = TT[tt_i]
                ab = actb[tt_i % 2]
                for ff in range(fg):
                    gb = psb[(2 * ff) % 4]
                    ub = psb[(2 * ff + 1) % 4]
                    gk = ("ps", (2 * ff) % 4)
                    uk = ("ps", (2 * ff + 1) % 4)
                    for k in range(KC):
                        B.mm(gb[:, 0:n], wb[:, k, 0, ff * 128:(ff + 1) * 128], hT[:, k, c0:c0 + n],
                             start=(k == 0), stop=(k == KC - 1),
                             reads=[("win_bf", slot), ("hT", tt_i)], writes=[gk])
                    for k in range(KC):
                        B.mm(ub[:, 0:n], wb[:, k, 1, ff * 128:(ff + 1) * 128], hT[:, k, c0:c0 + n],
                             start=(k == 0), stop=(k == KC - 1),
                             reads=[("win_bf", slot), ("hT", tt_i)], writes=[uk])
                    sg = sgb[ff % 2]
                    B.act(sg[:, 0:n], gb[:, 0:n], AF.Silu, reads=[gk], writes=[("sgb", ff % 2)])
                    B.tt("dve", ab[:, ff, 0:n], sg[:, 0:n], ub[:, 0:n], ALU.mult,
                         reads=[("sgb", ff % 2), uk], writes=[("actb", tt_i % 2)])

            def p2(tt_i):
                c0, n = TT[tt_i]
                ab = actb[tt_i % 2]
                xk = xkeys(c0, n)
                for d in range(KC):
                    yb = psb[4 + d % 2]
                    yk = ("ps", 4 + d % 2)
                    for ff in range(fg):
                        B.mm(yb[:, 0:n], ob[:, ff, d * 128:(d + 1) * 128], ab[:, ff, 0:n],
                             start=(ff == 0), stop=(ff == fg - 1),
                             reads=[("wout_bf", slot), ("actb", tt_i % 2)], writes=[yk])
                    B.stt(xT[:, d, c0:c0 + n], yb[:, 0:n], hgT[:, sl * 8 + d:sl * 8 + d + 1], xT[:, d, c0:c0 + n],
                          ALU.mult, ALU.add, reads=[yk, ("hgT", sl)] + xk, writes=xk)

            for ii, tt_i in enumerate(tiles):
                p1(tt_i)
                if ii > 0:
                    p2(tiles[ii - 1])
            p2(tiles[-1])
            for blk in hk:
                ada_mm(blk)

    mix_w = B.din("mix_w_in", [D, MIXW])
    convw_d = B.din("conv_wT", [P, 96])
    alog_d = B.din("alog_rep", [P, 128])
    dtb_d = B.din("dtb_rep", [P, 128])
    dng_d = B.din("dng", [P, 1])
    poolw_d = B.din("pool_w", [4, P, P])
    pscale_d = B.din("pool_scaleT", [P, 4])
    poolproj_d = B.din("pool_proj", [512, D])
    dnproj_d = B.din("dn_proj", [D, D])
    mixout_d = B.din("mix_w_out", [D, D])
    cst_d = B.din("cst", [P, 9 * 128])
    hmask_d = B.din("halo_mask", [P, 1])
    invdiv_d = B.din("invdiv", [P, 64])
    sx_in = nc.dram_tensor("sx_in", [8 * P, P], F32, kind="Internal").ap()
    sx_out = nc.dram_tensor("sx_out", [8 * 2 * P, P], F32, kind="Internal").ap()
    cc_sem_box = [None]

    so = [0]

    def sm(n):
        ap = small[:, so[0]:so[0] + n]
        so[0] += n
        assert so[0] <= 2048
        return ap
    convw = sm(96); alog = sm(128); dtb = sm(128); dng = sm(1); pscale = sm(4); hmask = sm(1); invdiv = sm(64)
    one_c = sm(1)
    beta = sm(128); gcs = sm(128); egc = sm(128); ekd = sm(128); eglA = sm(128); eglB = sm(128); bke = sm(128)
    gtmp = sm(128); gtmp2 = sm(128)
    B.dma(convw, convw_d, writes=["convw"]); B.dma(alog, alog_d, writes=["alog"]); B.dma(dtb, dtb_d, writes=["dtb"])
    B.dma(dng, dng_d, writes=["dng"]); B.dma(pscale, pscale_d, writes=["pscale"]); B.dma(hmask, hmask_d, writes=["hmask"])
    B.dma(invdiv, invdiv_d, writes=["invdiv"])
    B.memset("pool", one_c, 1.0, writes=["one_c"])

    def mixer():
        c = Carver(0)
        cstf = c.f32(9 * 128)
        B.dma(cstf, cst_d, writes=["cstf"])
        CM = lambda i: cstf[:, i * 128:(i + 1) * 128]
        mLs, mUi, m16, nm16, Lc, Lall, LA, LB, onesf = [CM(i) for i in range(9)]
        ident_bf = c.bf16(128)
        B.cp("pool", ident_bf, ident[:], reads=["ident"], writes=["ident_bf"])
        oT = c.bf16(8 * T).rearrange("p (h t) -> p h t", h=8)
        wst1 = c.f32(KC * 128).rearrange("p (k c) -> p k c", k=KC)
        wst = [wst1, wst1]
        wbf = [c.bf16(KC * 128).rearrange("p (k c) -> p k c", k=KC) for i in range(2)]
        mark = c.off
        wctr = [0]

        def load_w(dram, row0, col0, ncols, nk=KC):
            i = wctr[0] % 2
            wctr[0] += 1
            for k in range(nk):
                B.dma(wst[i][:, k, 0:ncols], dram[row0 + k * 128:row0 + (k + 1) * 128, col0:col0 + ncols],
                      writes=["wst"])
            B.cp("pool", wbf[i][:, 0:nk, 0:ncols], wst[i][:, 0:nk, 0:ncols], reads=["wst"], writes=[("wbf", i)])
            return wbf[i], ("wbf", i)

        def proj_gen(col0, tiles, out_fn, banks, ncols=128):
            w, wk = load_w(mix_w, 0, col0, ncols)
            yield
            for ii, tt_i in enumerate(tiles):
                c0, n = TT[tt_i]
                bi = banks[ii % len(banks)]
                for k in range(KC):
                    B.mm(psb[bi][0:ncols, 0:n], w[:, k, 0:ncols], hT[:, k, c0:c0 + n], start=(k == 0), stop=(k == KC - 1),
                         reads=[wk, ("hT", tt_i)], writes=[("ps", bi)])
                r = out_fn(tt_i, psb[bi][0:ncols, 0:n], ("ps", bi), c0, n)
                if r is not None:
                    yield from r
                yield

        def evac_masked(dst_fn, key):
            def f(tt_i, ps, pk, c0, n):
                if tt_i == 0:
                    B.act(dst_fn(c0, n), ps, AF.Copy, scale=hmask, reads=[pk, "hmask"], writes=[key])
                else:
                    B.cp("act" if tt_i % 2 else "dve", dst_fn(c0, n), ps, reads=[pk], writes=[key])
            return f

        def run(*gens):
            gens = list(gens)
            while gens:
                for g in list(gens):
                    try:
                        next(g)
                    except StopIteration:
                        gens.remove(g)

        for tt_i in range(len(TT)):
            def out_fn(k, t_ap, tkey, c0, n):
                B.act(hT[:, k, c0:c0 + n], t_ap, AF.Identity, bias=modT[:, 24 + k:24 + k + 1], scale=1.0,
                      reads=[tkey, "modT"], writes=[("hT", tt_i)])
            norm_tile(tt_i, lambda k: gsT[:, 8 + k:8 + k + 1], [("gsT", 1)], out_fn)

        wba_st = c.f32(KC * 16).rearrange("p (k c) -> p k c", k=KC)
        wba = c.bf16(KC * 16).rearrange("p (k c) -> p k c", k=KC)
        for k in range(KC):
            B.dma(wba_st[:, k, :], mix_w[k * 128:(k + 1) * 128, 4608:4624], writes=["wba_st"])
        B.cp("pool", wba, wba_st, reads=["wba_st"], writes=["wba"])
        for j in range(16):
            for k in range(KC):
                B.mm(psb[4][:, j * 16:(j + 1) * 16], hT[:, k, TH + j * 128:TH + (j + 1) * 128], wba[:, k, :],
                     start=(k == 0), stop=(k == KC - 1), reads=["wba"] + [("hT", t) for t in range(5)], writes=[("ps", 4)])
        pv = psb[4][:, 0:256].rearrange("p (j c) -> p j c", c=16)
        v3 = lambda ap: ap.rearrange("p (j h) -> p j h", h=8)
        B.act(v3(beta), pv[:, :, 0:8], AF.Sigmoid, reads=[("ps", 4)], writes=["beta"])
        B.tt("dve", v3(gtmp), pv[:, :, 8:16], v3(dtb), ALU.add, reads=[("ps", 4), "dtb"], writes=["gtmp"])
        B.act(gtmp, gtmp, AF.Exp, reads=["gtmp"], writes=["gtmp"])
        B.act(gtmp, gtmp, AF.Ln, bias=one_c, scale=1.0, reads=["gtmp", "one_c"], writes=["gtmp"])
        B.act(gtmp2, alog, AF.Exp, reads=["alog"], writes=["gtmp2"])
        B.stt(gtmp, gtmp, -1.0, gtmp2, ALU.mult, ALU.mult, reads=["gtmp", "gtmp2"], writes=["gtmp"])
        for i, Lm in enumerate((Lc, Lall, LA, LB)):
            B.mm(psb[5][:, i * 128:(i + 1) * 128], Lm, gtmp, reads=["cstf", "gtmp"], writes=[("ps", 5)])
        B.cp("act", gcs, psb[5][:, 0:128], reads=[("ps", 5)], writes=["gcs"])
        B.act(egc, psb[5][:, 0:128], AF.Exp, reads=[("ps", 5)], writes=["egc"])
        B.tt("dve", gtmp2, psb[5][:, 128:256], gcs, ALU.subtract, reads=[("ps", 5), "gcs"], writes=["gtmp2"])
        B.act(ekd, gtmp2, AF.Exp, reads=["gtmp2"], writes=["ekd"])
        B.act(eglA, psb[5][:, 256:384], AF.Exp, reads=[("ps", 5)], writes=["eglA"])
        B.act(eglB, psb[5][:, 384:512], AF.Exp, reads=[("ps", 5)], writes=["eglB"])
        B.tt("dve", bke, beta, egc, ALU.mult, reads=["beta", "egc"], writes=["bke"])

        c.off = mark
        s.barrier()
        pre = c.bf16(TE)
        dg = c.bf16(4 * 128).rearrange("p (j c) -> p j c", j=4)
        qT = c.bf16(T); kT = c.bf16(T); vT = c.bf16(T); QtT = c.bf16(T)
        Sx = c.f32(128); S0 = Sx; S0b = c.bf16(128); Smidb = c.bf16(128)
        PTA = c.bf16(128); PTB = c.bf16(128)

        class Lane:
            pass
        lanes = []
        for L in range(2):
            ln = Lane()
            ln.L = L
            ln.K = lambda name, L=L: f"{name}@{L}"
            ln.kvtok = c.bf16(256); ln.ktok = ln.kvtok[:, 0:128]; ln.vtok = ln.kvtok[:, 128:256]
            ln.gcd = c.f32(128); ln.tL = c.f32(128); ln.tU = c.f32(128); ln.egb = c.bf16(128)
            ln.Am = c.bf16(128); ln.Um = c.bf16(128); ln.EU = c.bf16(128); ln.QKm = c.bf16(128)
            ln.XY = [c.bf16(256) for i in range(2)]
            ln.Xb = [xy[:, 0:128] for xy in ln.XY]; ln.Yb = [xy[:, 128:256] for xy in ln.XY]; ln.RT = c.bf16(128)
            ln.rf = c.f32(256); ln.rb = c.bf16(256); ln.yb_ = ln.rb
            ln.uext = c.f32(256); ln.wbf_ = None; ln.wT = c.bf16(128); ln.kdec = c.bf16(128); ln.qdec = c.bf16(128)
            ln.GT = ln.Um; ln.y0wT = ln.tL
            ln.uext2 = [ln.uext, c.f32(256)]
            ln.wT2 = [ln.wT, c.bf16(128)]; ln.kdec2 = [ln.kdec, c.bf16(128)]
            ln.qdec2 = [ln.qdec, c.bf16(128)]; ln.QKm2 = [ln.QKm, c.bf16(128)]
            ln.Z = c.f32(256); ln.Zb = c.bf16(256); ln.vn = c.bf16(256)
            ln.bk = [4 * L + i for i in range(4)]
            B.memset("pool", ln.uext2[0][:, 128:256], 0.0, writes=[ln.K("uext0")])
            B.memset("pool", ln.uext2[1][:, 128:256], 0.0, writes=[ln.K("uext1")])
            lanes.append(ln)

        def h1_gen(h):
            sqb = sq[:, 0, :]; rinv = rstd[0][:]
            ksq, krinv = ("sq", 0), ("rstd", 0)
            P2, P3 = psb[2], psb[3]
            k2, k3 = ("ps", 2), ("ps", 3)
            for ci, (colbase, dstT) in enumerate(((512, qT), (1536, kT), (2560, vT))):
                chunk = ci * 8 + h
                for j in range(4):
                    B.act(dg[:, j, :], ident_bf, AF.Copy, scale=convw[:, chunk * 4 + j:chunk * 4 + j + 1],
                          reads=["ident_bf", "convw"], writes=["dg"])
                yield

                def stage_a(tt_i, ci=ci, dstT=dstT):
                    c0, n = TT[tt_i]
                    for j in range(4):
                        B.mm(P2[:, 0:n], dg[:, j, :], pre[:, c0 - 3 + j:c0 - 3 + j + n], start=(j == 0), stop=(j == 3),
                             reads=["dg", ("pre", tt_i - 1), ("pre", tt_i)], writes=[k2])
                    o0 = c0 - TH
                    qk = ("qkv", ci, tt_i)
                    B.act(dstT[:, o0:o0 + n], P2[:, 0:n], AF.Silu, reads=[k2], writes=[qk])
                    yield

                def stage_b(tt_i, ci=ci, dstT=dstT):
                    c0, n = TT[tt_i]
                    o0 = c0 - TH
                    qk = ("qkv", ci, tt_i)
                    B.act(sqb, dstT[:, o0:o0 + n], AF.Square, reads=[qk], writes=[ksq])
                    B.mm(P3[:, 0:n], ones_bf[:], sqb, reads=[ksq, "ones"], writes=[k3])
                    yield
                    B.act(rinv, P3[:, 0:n], AF.Ln, bias=eps_ap, scale=1.0, reads=[k3, "eps"], writes=[krinv])
                    B.act(rinv, rinv, AF.Exp, scale=-0.5, reads=[krinv], writes=[krinv])
                    yield
                    B.stt(dstT[:, o0:o0 + n], dstT[:, o0:o0 + n], (128.0 ** -0.5) if ci == 0 else 1.0, rinv,
                          ALU.mult, ALU.mult, reads=[qk, krinv], writes=[qk])
                    yield

                pend = []

                def out_fn(tt_i, ps, pk, c0, n, ci=ci):
                    if tt_i == 0:
                        B.act(pre[:, c0:c0 + n], ps, AF.Copy, scale=hmask, reads=[pk, "hmask"], writes=[("pre", 0)])
                    else:
                        B.cp("act" if tt_i % 2 else "dve", pre[:, c0:c0 + n], ps, reads=[pk], writes=[("pre", tt_i)])
                    todo = list(pend)
                    del pend[:]
                    if tt_i >= 1:
                        pend.append(("a", tt_i))

                    def emit():
                        for kind, t in todo:
                            if kind == "a":
                                yield from stage_a(t)
                                if ci < 2:
                                    pend.append(("b", t))
                            else:
                                yield from stage_b(t)
                    return emit()
                yield from proj_gen(colbase + h * 128, range(len(TT)), out_fn, [0, 1])
                while pend:
                    todo = list(pend)
                    del pend[:]
                    for kind, t in todo:
                        if kind == "a":
                            yield from stage_a(t)
                            if ci < 2:
                                pend.append(("b", t))
                        else:
                            yield from stage_b(t)

        def prep_tile(h, ln, j, jlo, jhi):
            K = ln.K
            P0, P1, P2, P3 = [psb[b] for b in ln.bk]
            k0, k1, k2, k3 = [("ps", b) for b in ln.bk]
            ktok, vtok, gcd, tL, tU, egb = ln.ktok, ln.vtok, ln.gcd, ln.tL, ln.tU, ln.egb
            DL, DU = tL, tU
            Am, Um, EU, QKm, Xb, Yb, RT = ln.Am, ln.Um, ln.EU, ln.QKm, ln.Xb, ln.Yb, ln.RT
            rf, rb, yb_, uext, wbf_, wT, kdec, qdec = ln.rf, ln.rb, ln.yb_, ln.uext, ln.wbf_, ln.wT, ln.kdec, ln.qdec
            Z, Zb, vn = ln.Z, ln.Zb, ln.vn
            bsel = j % 2
            uext, wT, kdec, qdec, QKm = ln.uext2[bsel], ln.wT2[bsel], ln.kdec2[bsel], ln.qdec2[bsel], ln.QKm2[bsel]
            Kb = lambda name: K(f"{name}{bsel}")
            if j == jlo:
                B.ts("pool", gcd, ident[:], gcs[:, j * 8 + h:j * 8 + h + 1], None, ALU.mult,
                     reads=["ident", "gcs"], writes=[K("gcd")])
            col = j * 8 + h
            t0 = j * 128
            tt_i = 1 + j // 4
            qk_keys = [("qkv", 0, tt_i), ("qkv", 1, tt_i), ("qkv", 2, tt_i)]
            kTt = kT[:, t0:t0 + 128]; qTt = qT[:, t0:t0 + 128]; vTt = vT[:, t0:t0 + 128]
            gp = gcs[:, col:col + 1]
            B.mm(P0[:, 0:128], kTt, ident_bf, reads=qk_keys + ["ident_bf"], writes=[k0])
            B.mm(P0[:, 128:256], vTt, ident_bf, reads=qk_keys + ["ident_bf"], writes=[k0])
            yield
            B.cp("act", ln.kvtok, P0[:, 0:256], reads=[k0], writes=[K("ktok"), K("vtok")])
            B.mm(P1[:, 0:128], onesf, gcd, reads=["cstf", K("gcd")], writes=[k1])
            if j + 1 < jhi:
                ncol = (j + 1) * 8 + h
                B.ts("pool", gcd, ident[:], gcs[:, ncol:ncol + 1], None, ALU.mult,
                     reads=["ident", "gcs"], writes=[K("gcd")])
            B.mm(P1[:, 128:256], kTt, kTt, reads=qk_keys, writes=[k1])
            B.mm(P1[:, 256:384], kTt, qTt, reads=qk_keys, writes=[k1])
            yield
            B.stt(tL, P1[:, 0:128], gp, mLs, ALU.subtract, ALU.max, reads=[k1, "gcs", "cstf"], writes=[K("tL")])
            B.stt(tU, P1[:, 0:128], gp, mUi, ALU.subtract, ALU.min, reads=[k1, "gcs", "cstf"], writes=[K("tU")])
            yield
            B.act(egb, P1[:, 0:128], AF.Exp, reads=[k1], writes=[K("egb")])
            B.act(DL, tL, AF.Exp, scale=-1.0, reads=[K("tL")], writes=[K("tL")])
            B.act(DU, tU, AF.Exp, reads=[K("tU")], writes=[K("tU")])
            yield
            B.stt(Am, P1[:, 128:256], beta[:, col:col + 1], DL, ALU.mult, ALU.mult,
                  reads=[k1, "beta", K("tL")], writes=[K("Am")])
            B.tt("dve", QKm, P1[:, 256:384], DU, ALU.mult, reads=[k1, K("tU")], writes=[Kb("QKm")])
            B.act(rb[:, 0:128], vtok, AF.Copy, scale=beta[:, col:col + 1], reads=[K("vtok"), "beta"], writes=[K("rb")])
            B.act(rb[:, 128:256], ktok, AF.Copy, scale=bke[:, col:col + 1], reads=[K("ktok"), "bke"], writes=[K("rb")])
            yield
            B.mm(P0[:, 256:384], Am, ident_bf, reads=[K("Am"), "ident_bf"], writes=[k0])
            X0, Y0 = Xb[0], Yb[0]
            B.stt(X0, Am, -1.0, m16, ALU.mult, ALU.mult, reads=[K("Am"), "cstf"], writes=[(K("X"), 0)])
            yield
            B.cp("act", Um, P0[:, 256:384], reads=[k0], writes=[K("Um")])
            yield
            B.stt(Y0, Um, -1.0, m16, ALU.mult, ALU.mult, reads=[K("Um"), "cstf"], writes=[(K("Y"), 0)])
            B.tt("dve", EU, Am, nm16, ALU.mult, reads=[K("Am"), "cstf"], writes=[K("EU")])
            B.stt(RT, Y0, 1.0, ident[:], ALU.mult, ALU.add, reads=[(K("Y"), 0), "ident"], writes=[K("RT")])
            yield
            cur = 0
            for lvl in range(3):
                nx = cur ^ 1
                B.mm(P0[:, 0:128], Yb[cur], Xb[cur], reads=[(K("X"), cur), (K("Y"), cur)], writes=[k0])
                if lvl < 2:
                    B.mm(P0[:, 128:256], Xb[cur], Yb[cur], reads=[(K("X"), cur), (K("Y"), cur)], writes=[k0])
                yield
                if lvl < 2:
                    B.cp("act", ln.XY[nx], P0[:, 0:256], reads=[k0], writes=[(K("X"), nx), (K("Y"), nx)])
                else:
                    B.cp("act", Xb[nx], P0[:, 0:128], reads=[k0], writes=[(K("X"), nx)])
                yield
                B.mm(P1[:, 0:128], Xb[nx], RT, reads=[(K("X"), nx), K("RT")], writes=[k1])
                yield
                B.tt("dve", RT, RT, P1[:, 0:128], ALU.add, reads=[K("RT"), k1], writes=[K("RT")])
                yield
                cur = nx
            GT, y0wT = ln.GT, ln.y0wT
            B.mm(P0[:, 0:256], RT, rb, reads=[K("RT"), K("rb")], writes=[k0])
            B.mm(P1[:, 0:128], EU, RT, reads=[K("EU"), K("RT")], writes=[k1])
            B.mm(P1[:, 128:256], rb[:, 128:256], RT, reads=[K("RT"), K("rb")], writes=[k1])
            B.ts("pool", kdec, ktok, ekd[:, col:col + 1], None, ALU.mult, reads=[K("ktok"), "ekd"], writes=[Kb("kdec")])
            yield
            B.cp("act", yb_, P0[:, 0:256], reads=[k0], writes=[K("rb")])
            B.cp("act", GT, P1[:, 0:128], reads=[k1], writes=[K("Um")])
            B.cp("dve", rf, P0[:, 0:256], reads=[k0], writes=[K("rf")])
            B.cp("act", y0wT, P1[:, 128:256], reads=[k1], writes=[K("tL")])
            B.tt("dve", qdec, qTt, egb, ALU.mult, reads=qk_keys + [K("egb")], writes=[Kb("qdec")])
            yield
            for it in range(3):
                if it < 2:
                    B.mm(P0[:, 0:256], GT, yb_, reads=[K("Um"), K("rb")], writes=[k0])
                    yield
                    B.tt("dve", yb_, rf, P0[:, 0:256], ALU.subtract, reads=[K("rf"), k0], writes=[K("rb")])
                    yield
                else:
                    B.mm(P0[:, 0:128], GT, yb_[:, 0:128], reads=[K("Um"), K("rb")], writes=[k0])
                    B.mm(P0[:, 128:256], yb_[:, 128:256], GT, reads=[K("Um"), K("rb")], writes=[k0])
                    yield
                    B.tt("dve", wT, y0wT, P0[:, 128:256], ALU.subtract, reads=[K("tL"), k0], writes=[Kb("wT")])
                    B.tt("dve", uext[:, 0:128], rf[:, 0:128], P0[:, 0:128], ALU.subtract,
                         reads=[K("rf"), k0], writes=[Kb("uext")])
                    yield

        def recur_tile(h, ln, j, jlo, jhi):
            K = ln.K
            P0, P1, P2, P3 = [psb[b] for b in ln.bk]
            k0, k1, k2, k3 = [("ps", b) for b in ln.bk]
            ktok, vtok, gcd, tL, tU, egb = ln.ktok, ln.vtok, ln.gcd, ln.tL, ln.tU, ln.egb
            DL, DU = tL, tU
            Am, Um, EU, QKm, Xb, Yb, RT = ln.Am, ln.Um, ln.EU, ln.QKm, ln.Xb, ln.Yb, ln.RT
            rf, rb, yb_, uext, wbf_, wT, kdec, qdec = ln.rf, ln.rb, ln.yb_, ln.uext, ln.wbf_, ln.wT, ln.kdec, ln.qdec
            Z, Zb, vn = ln.Z, ln.Zb, ln.vn
            bsel = j % 2
            uext, wT, kdec, qdec, QKm = ln.uext2[bsel], ln.wT2[bsel], ln.kdec2[bsel], ln.qdec2[bsel], ln.QKm2[bsel]
            Kb = lambda name: K(f"{name}{bsel}")
            col = j * 8 + h
            if j == jlo:
                B.memset("pool", Z[:, 0:128], 0.0, writes=[K("Z")])
                B.cp("pool", Z[:, 128:256], ident[:], reads=["ident"], writes=[K("Z")])
                B.cp("act", Zb, Z, reads=[K("Z")], writes=[K("Zb")])
                yield
            for cc in range(2):
                r0 = cc * 64
                pc = (j % 2) * 128 + r0
                egl = (eglA if cc == 0 else eglB)[:, col:col + 1]
                B.mm(P3[r0:r0 + 64, 0:256], wT[:, r0:r0 + 64], Zb, reads=[Kb("wT"), K("Zb")], writes=[k3])
                B.mm(P2[:, pc:pc + 64], Zb[:, 0:128], qdec[:, r0:r0 + 64], start=True, stop=False,
                     reads=[K("Zb"), Kb("qdec")], writes=[k2])
                yield
                B.tt("dve", vn[r0:r0 + 64, :], uext[r0:r0 + 64, :], P3[r0:r0 + 64, 0:256], ALU.subtract,
                     reads=[Kb("uext"), k3], writes=[K("vn")])
                yield
                B.mm(P2[:, pc:pc + 64], vn[r0:r0 + 64, 0:128], QKm[r0:r0 + 64, r0:r0 + 64], start=False, stop=True,
                     reads=[K("vn"), Kb("QKm")], writes=[k2])
                B.mm(P3[:, 256:512], kdec[r0:r0 + 64, :], vn[r0:r0 + 64, :], reads=[Kb("kdec"), K("vn")], writes=[k3])
                B.mm(P2[:, 256 + pc:256 + pc + 64], Zb[:, 128:256], qdec[:, r0:r0 + 64], start=True, stop=False,
                     reads=[K("Zb"), Kb("qdec")], writes=[k2])
                B.mm(P2[:, 256 + pc:256 + pc + 64], vn[r0:r0 + 64, 128:256], QKm[r0:r0 + 64, r0:r0 + 64], start=False, stop=True,
                     reads=[K("vn"), Kb("QKm")], writes=[k2])
                yield
                B.stt(Zb, Z, egl, P3[:, 256:512], ALU.mult, ALU.add, reads=[K("Z"), k3, "eglA", "eglB"], writes=[K("Zb")])
                B.stt(Z, Z, egl, P3[:, 256:512], ALU.mult, ALU.add, reads=[K("Z"), k3, "eglA", "eglB"], writes=[K("Z")])
                yield
            if j % 2 == 1:
                o0 = (j // 2) * 256
                B.cp("act", oT[:, h, o0:o0 + 256], P2[:, 0:256], reads=[k2], writes=[("oT", h, j // 4)])
                B.cp("dve", QtT[:, o0:o0 + 256], P2[:, 256:512], reads=[k2], writes=[("QtT", j // 4)])
                yield

        def fin_gen(h):
            lA, lB = lanes
            P6, P7 = psb[6], psb[7]
            k6, k7 = ("ps", 6), ("ps", 7)
            sqb = sq[:, 1, :]; rinv = tmpf[1][:]; szb = rstd[1][:]; otmp = tmpf[0][:]
            ksq, krinv, kszb, kotmp = ("sq", 1), ("tmpf", 1), ("rstd", 1), ("tmpf", 0)
            B.mm(P6[:, 0:128], lA.Zb[:, 128:256], ident_bf, reads=[lA.K("Zb"), "ident_bf"], writes=[k6])
            B.mm(P6[:, 128:256], lB.Zb[:, 128:256], ident_bf, reads=[lB.K("Zb"), "ident_bf"], writes=[k6])
            yield
            B.cp("act", PTA, P6[:, 0:128], reads=[k6], writes=["PTA"])
            B.cp("act", PTB, P6[:, 128:256], reads=[k6], writes=["PTB"])
            yield
            B.mm(P6[:, 256:384], PTB, lA.Zb[:, 0:128], reads=["PTB", lA.K("Zb")], writes=[k6])
            yield
            B.tt("dve", Sx, P6[:, 256:384], lB.Z[:, 0:128], ALU.add, reads=[k6, lB.K("Z")], writes=["Sx"])
            yield
            B.dma(sx_in[h * P:(h + 1) * P, :], Sx, reads=["Sx"], writes=[("sx_in", h)])
            if dbg.get("nocc"):
                B.dma(sx_out[h * 2 * P:h * 2 * P + P, :], sx_in[h * P:(h + 1) * P, :], reads=[("sx_in", h)], writes=[("sx_out", h)])
            else:
                s.add("pool", lambda e, h=h: e.collective_compute(
                    "AllGather", ALU.bypass, replica_groups=[[0, 1], [2, 3], [4, 5], [6, 7]],
                    ins=[sx_in[h * P:(h + 1) * P, :]], outs=[sx_out[h * 2 * P:(h + 1) * 2 * P, :]]),
                    [("sx_in", h)], [("sx_out", h)], dma=True, inc=CC_INC)
            B.dma(S0, sx_out[h * 2 * P:h * 2 * P + P, :], reads=[("sx_out", h)], writes=["Sx"])
            yield
            B.ts("dve", S0b, S0, hmask, None, ALU.mult, reads=["Sx", "hmask"], writes=["S0b"])
            yield
            B.mm(P6[:, 384:512], PTA, S0b, reads=["PTA", "S0b"], writes=[k6])
            yield
            B.tt("dve", Smidb, P6[:, 384:512], lA.Z[:, 0:128], ALU.add, reads=[k6, lA.K("Z")], writes=["Smidb"])
            yield

            def z_out(tt_i, ps, pk, c0, n):
                o0 = c0 - TH
                Sc, Sk = (S0b, "S0b") if tt_i <= 2 else (Smidb, "Smidb")
                B.act(szb, ps, AF.Silu, reads=[pk], writes=[kszb])
                B.mm(P6[:, 0:n], Sc, QtT[:, o0:o0 + n], reads=[Sk, ("QtT", tt_i - 1)], writes=[k6])
                yield
                B.tt("dve", otmp, P6[:, 0:n], oT[:, h, o0:o0 + n], ALU.add, reads=[k6, ("oT", h, tt_i - 1)], writes=[kotmp])
                yield
                B.act(sqb, otmp, AF.Square, reads=[kotmp], writes=[ksq])
                B.mm(P7[:, 0:n], ones_bf[:], sqb, reads=[ksq, "ones"], writes=[k7])
                yield
                B.act(rinv, P7[:, 0:n], AF.Ln, bias=eps_ap, scale=1.0 / 128, reads=[k7, "eps"], writes=[krinv])
                B.act(rinv, rinv, AF.Exp, scale=-0.5, reads=[krinv], writes=[krinv])
                yield
                B.stt(otmp, otmp, dng, rinv, ALU.mult, ALU.mult, reads=[kotmp, "dng", krinv], writes=[kotmp])
                B.tt("dve", oT[:, h, o0:o0 + n], otmp, szb, ALU.mult, reads=[kotmp, kszb], writes=[("oT", h, tt_i - 1)])
                yield
            yield from proj_gen(3584 + h * 128, range(1, 5), z_out, [4, 5])

        run(h1_gen(0))
        for h in range(8):
            lA_, lB_ = lanes
            run(prep_tile(h, lA_, 0, 0, 8), prep_tile(h, lB_, 8, 8, 16))
            for t in range(8):
                gens = []
                if t + 1 < 8:
                    gens += [prep_tile(h, lA_, t + 1, 0, 8), prep_tile(h, lB_, 8 + t + 1, 8, 16)]
                gens += [recur_tile(h, lA_, t, 0, 8), recur_tile(h, lB_, 8 + t, 8, 16)]
                run(*gens)
            if h < 7:
                run(h1_gen(h + 1), fin_gen(h))
            else:
                run(fin_gen(h))

        c.off = mark
        s.barrier()
        xpT = c.bf16(4 * TE).rearrange("p (g t) -> p g t", g=4)
        mark2 = c.off
        for g in range(4):
            run(proj_gen(g * 128, range(len(TT)), evac_masked(lambda c0, n, g=g: xpT[:, g, c0:c0 + n], ("xpT", g)), [0, 1]))
        poolw_st = c.f32(4 * 128).rearrange("p (g c) -> p g c", g=4)
        poolw_bf = c.bf16(4 * 128).rearrange("p (g c) -> p g c", g=4)
        for g in range(4):
            B.dma(poolw_st[:, g, :], poolw_d[g], writes=["poolw_st"])
        B.cp("pool", poolw_bf, poolw_st, reads=["poolw_st"], writes=["poolw_bf"])
        sA = c.f32(528); sB = c.f32(528); pooled = [c.bf16(512) for i in range(2)]
        WINS = (2, 4, 8, 16)
        for tt_i in (4, 3, 2, 1):
            c0, n = TT[tt_i]
            for g in range(4):
                w = WINS[g]
                src = xpT[:, g, c0 - 16:c0 + n]
                cur, ckey = src, ("xpT", g)
                lo = 0
                bufs = [(sA, "sA"), (sB, "sB")]
                bi = 0
                sh = 1
                while sh < w:
                    dst, dkey = bufs[bi]
                    bi ^= 1
                    nlo = lo + sh
                    B.tt("pool" if g % 2 else "dve", dst[:, nlo:528], cur[:, nlo:528], cur[:, nlo - sh:528 - sh], ALU.add,
                         reads=[ckey], writes=[dkey])
                    cur, ckey, lo = dst, dkey, nlo
                    sh *= 2
                pb = pooled[g % 2]
                pk = ("pooled", g % 2)
                s.add("dve", lambda e, pb=pb, cur=cur, src=src, w=w, n=n: e.scalar_tensor_tensor(
                    pb[:, 0:n], cur[:, 16:16 + n], 1.0 / w, src[:, 16:16 + n], ALU.mult, ALU.subtract),
                    [ckey, ("xpT", g)], [pk])
                if tt_i == 1:
                    oth, okey = (sB, "sB") if cur is sA else (sA, "sA")
                    B.tt("dve", oth[:, 0:16], cur[:, 16:32], invdiv[:, g * 16:(g + 1) * 16], ALU.mult,
                         reads=[ckey, "invdiv"], writes=[okey])
                    B.tt("dve", pb[:, 0:16], oth[:, 0:16], src[:, 16:32], ALU.subtract,
                         reads=[okey, ("xpT", g)], writes=[pk])
                B.mm(psb[2 + g % 2][:, 0:n], poolw_bf[:, g, :], pb[:, 0:n], reads=["poolw_bf", pk], writes=[("ps", 2 + g % 2)])
                B.act(xpT[:, g, c0:c0 + n], psb[2 + g % 2][:, 0:n], AF.Copy, scale=pscale[:, g:g + 1],
                      reads=[("ps", 2 + g % 2), "pscale", "sA", "sB"], writes=[("xpT", g)])
        yaT = xpT
        c.off = mark2
        s.barrier()
        dnw = [c.bf16(8 * 128).rearrange("p (g c) -> p g c", g=8) for i in range(2)]
        ppw = [c.bf16(4 * 128).rearrange("p (g c) -> p g c", g=4) for i in range(2)]
        mow = [c.bf16(D) for i in range(2)]
        wgpb = [c.bf16(8 * 128).rearrange("p (g c) -> p g c", g=8) for i in range(2)]
        wgdb = [c.bf16(8 * 128).rearrange("p (g c) -> p g c", g=8) for i in range(2)]
        tsts = [c.f32(KC * 128) for i in range(2)]
        mrg = [c.bf16(512) for i in range(2)]
        sg1 = rstd[0][:]; sg2 = rstd[1][:]; m1 = tmpf[0][:]
        tctr = [0]

        def stage(dst, srcs, rows3=True):
            i = tctr[0] % 2
            tctr[0] += 1
            tst = tsts[i]
            tst3 = tst.rearrange("p (k c) -> p k c", k=KC)
            if rows3:
                for k, src in enumerate(srcs):
                    B.dma(tst3[:, k, :], src, writes=[("tst", i, k)])
                B.cp("pool", dst[0], tst3[:, 0:len(srcs), :], reads=[("tst", i, k) for k in range(len(srcs))],
                     writes=[dst[1]] + [("tst", i, k) for k in range(KC)])
            else:
                B.dma(tst, srcs, writes=[("tst", i, k) for k in range(KC)])
                B.cp("pool", dst[0], tst, reads=[("tst", i, k) for k in range(KC)],
                     writes=[dst[1]] + [("tst", i, k) for k in range(KC)])

        def load_tail(e_):
            sl = e_ % 2
            esl = slice(e_ * 128, (e_ + 1) * 128)
            stage((ppw[sl], ("ppw", sl)), [poolproj_d[k * 128:(k + 1) * 128, esl] for k in range(4)])
            stage((wgpb[sl], ("wgp", sl)), [mix_w[k * 128:(k + 1) * 128, 4624 + e_ * 128:4624 + (e_ + 1) * 128] for k in range(KC)])
            stage((dnw[sl], ("dnw", sl)), [dnproj_d[k * 128:(k + 1) * 128, esl] for k in range(KC)])
            stage((wgdb[sl], ("wgd", sl)), [mix_w[k * 128:(k + 1) * 128, 5648 + e_ * 128:5648 + (e_ + 1) * 128] for k in range(KC)])
            stage((mow[sl], ("mow", sl)), mixout_d[esl, :], rows3=False)

        def projpart(e_, tt_i):
            sl = e_ % 2
            c0, n = TT[tt_i]
            o0 = c0 - TH
            for g in range(4):
                B.mm(psb[0][:, 0:n], ppw[sl][:, g, :], yaT[:, g, c0:c0 + n], start=(g == 0), stop=(g == 3),
                     reads=[("ppw", sl)] + [("xpT", gg) for gg in range(4)], writes=[("ps", 0)])
            for k in range(KC):
                B.mm(psb[1][:, 0:n], wgpb[sl][:, k, :], hT[:, k, c0:c0 + n], start=(k == 0), stop=(k == KC - 1),
                     reads=[("wgp", sl), ("hT", tt_i)], writes=[("ps", 1)])
            for hh in range(8):
                B.mm(psb[2][:, 0:n], dnw[sl][:, hh, :], oT[:, hh, o0:o0 + n], start=(hh == 0), stop=(hh == 7),
                     reads=[("dnw", sl)] + [("oT", hh, tt_i - 1) for hh in range(8)], writes=[("ps", 2)])
            for k in range(KC):
                B.mm(psb[3][:, 0:n], wgdb[sl][:, k, :], hT[:, k, c0:c0 + n], start=(k == 0), stop=(k == KC - 1),
                     reads=[("wgd", sl), ("hT", tt_i)], writes=[("ps", 3)])
            B.act(sg1, psb[1][:, 0:n], AF.Sigmoid, reads=[("ps", 1)], writes=[("rstd", 0)])
            B.act(sg2, psb[3][:, 0:n], AF.Sigmoid, reads=[("ps", 3)], writes=[("rstd", 1)])
            B.tt("dve", m1, sg1, psb[0][:, 0:n], ALU.mult, reads=[("rstd", 0), ("ps", 0)], writes=[("tmpf", 0)])
            B.tt("dve", sg2, sg2, psb[2][:, 0:n], ALU.mult, reads=[("rstd", 1), ("ps", 2)], writes=[("rstd", 1)])
            mb = mrg[tt_i % 2]
            B.tt("dve", mb[:, 0:n], sg2, m1, ALU.add, reads=[("rstd", 1), ("tmpf", 0)], writes=[("mrg", tt_i % 2)])

        def mowpart(e_, tt_i):
            sl = e_ % 2
            c0, n = TT[tt_i]
            xk = xkeys(c0, n)
            mb = mrg[tt_i % 2]
            for d in range(KC):
                bi = 4 + d % 4
                B.mm(psb[bi][:, 0:n], mow[sl][:, d * 128:(d + 1) * 128], mb[:, 0:n],
                     reads=[("mow", sl), ("mrg", tt_i % 2)], writes=[("ps", bi)])
                B.stt(xT[:, d, c0:c0 + n], psb[bi][:, 0:n], hgT[:, 8 + d:8 + d + 1], xT[:, d, c0:c0 + n],
                      ALU.mult, ALU.add, reads=[("ps", bi), ("hgT", 1)] + xk, writes=xk)

        load_tail(0)
        prev = None
        for e_ in range(KC):
            for tt_i in range(1, 5):
                projpart(e_, tt_i)
                if prev is not None:
                    mowpart(*prev)
                if tt_i == 1 and e_ + 1 < KC:
                    load_tail(e_ + 1)
                prev = (e_, tt_i)
        mowpart(*prev)

    if dbg.get("ffn1", True):
        per = (len(ada_rest) + len(GROUPS) - 1) // len(GROUPS)
        ffn(0, w1_in, w1_out, [0, 1, 2, 3, 4], hook=lambda gi: ada_rest[gi * per:(gi + 1) * per])
    else:
        for blk in ada_rest:
            ada_dma(blk)
            ada_mm(blk)
    ada_finish([1, 2])
    if dbg.get("mixer", True):
        s.barrier()
        mixer()
        s.barrier()
    if dbg.get("ffn2", True):
        ffn(2, w2_in, w2_out, [1, 2, 3, 4])
    s.barrier()

    outb = [scr[:, 4096 + i * 1024:4096 + (i + 1) * 1024] for i in range(2)]
    yT = scr[:, 0:4096].rearrange("p (k c) -> p k c", k=KC)
    for tt_i in range(1, len(TT)):
        def out_fn(k, t_ap, tkey, c0, n):
            B.cp("act", yT[:, k, 0:n], t_ap, reads=[tkey], writes=[("yT", k), ("scr", 0), ("scr", 1), ("scr", 2)])
        norm_tile(tt_i, lambda k: norm_gT[:, 24 + k:24 + k + 1], ["norm_gT"], out_fn)
        c0, n = TT[tt_i]
        for sub in range(4):
            ob_i = (tt_i * 4 + sub) % 2
            ob = outb[ob_i]
            for half in range(2):
                bank = psb[half]
                bkey = ("ps", half)
                for j in range(4):
                    k = half * 4 + j
                    B.tr(bank[:, j * 128:(j + 1) * 128], yT[:, k, sub * 128:(sub + 1) * 128], ident[:],
                         reads=[("yT", k), "ident"], writes=[bkey])
                B.cp("act" if half == 0 else "dve", ob[:, half * 512:(half + 1) * 512], bank[:],
                     reads=[bkey], writes=[("outb", ob_i), ("scr", 3)])
            r0 = (tt_i - 1) * 512 + sub * 128
            B.dma(out_d[r0:r0 + 128, :], ob[:], reads=[("outb", ob_i)], writes=[("outd", r0)])

    return finish(B)


_CACHE = {}


def _layout_T(v, nchunk):
    return np.ascontiguousarray(v.reshape(nchunk, P).T)


def kernel(**inp):
    x = np.asarray(inp["x"], np.float32)
    c = np.asarray(inp["c"], np.float32)
    if "B" not in _CACHE:
        _CACHE["B"] = build()
    B = _CACHE["B"]
    ada_bT = _layout_T(np.asarray(inp["ada_b"], np.float32)[0], 72)
    ng = np.asarray(inp["norm_g"], np.float32)[0]
    norm_gT = np.concatenate([_layout_T(ng[i], 8) for i in range(3)] +
                             [_layout_T(np.asarray(inp["final_g"], np.float32), 8)], axis=1)
    ident = np.eye(P, dtype=np.float32)
    shared = {
        "ada_w": np.ascontiguousarray(np.asarray(inp["ada_w"], np.float32)[0]),
        "ada_bT": np.ascontiguousarray(ada_bT),
        "norm_gT": np.ascontiguousarray(norm_gT),
        "ident": ident,
        "ffn1_w_in": np.ascontiguousarray(np.asarray(inp["ffn1_w_in"], np.float32)[0]),
        "ffn1_w_out": np.ascontiguousarray(np.asarray(inp["ffn1_w_out"], np.float32)[0]),
        "ffn2_w_in": np.ascontiguousarray(np.asarray(inp["ffn2_w_in"], np.float32)[0]),
        "ffn2_w_out": np.ascontiguousarray(np.asarray(inp["ffn2_w_out"], np.float32)[0]),
    }
    f32 = lambda k: np.asarray(inp[k], np.float32)
    cw = f32("conv_w")[0]
    conv_wT = np.ascontiguousarray(cw.reshape(4, 24, P).transpose(2, 1, 0).reshape(P, 96))
    idx = np.arange(P)
    blk64 = (idx[:, None] // 64) == (idx[None, :] // 64)
    blk16 = (idx[:, None] // 16) == (idx[None, :] // 16)
    mLs = blk64 & (idx[:, None] > idx[None, :])
    mUi = blk64 & (idx[None, :] >= idx[:, None])
    Lc = blk64 & (idx[:, None] <= idx[None, :])
    LA = np.broadcast_to((idx[:, None] < 64), (P, P))
    LB = np.broadcast_to((idx[:, None] >= 64), (P, P))
    cst = np.concatenate([m.astype(np.float32) for m in
                          (np.where(mLs, 0.0, 1e5), np.where(mUi, 0.0, -1e5), blk16, ~blk16, Lc, blk64, LA, LB,
                           np.ones((P, P), bool))], axis=1)
    shared.update({
        "mix_w_in": np.ascontiguousarray(f32("mix_w_in")[0]),
        "conv_wT": conv_wT,
        "alog_rep": np.ascontiguousarray(np.tile(f32("a_log")[0][None, None, :], (P, 16, 1)).reshape(P, 128)),
        "dtb_rep": np.ascontiguousarray(np.tile(f32("dt_bias")[0][None, None, :], (P, 16, 1)).reshape(P, 128)),
        "dng": np.ascontiguousarray(f32("dn_norm_g")[0].reshape(P, 1)),
        "pool_w": np.ascontiguousarray(f32("pool_w")[0]),
        "pool_scaleT": _layout_T(f32("pool_scale")[0], 4),
        "pool_proj": np.ascontiguousarray(f32("pool_proj")[0]),
        "dn_proj": np.ascontiguousarray(f32("dn_proj")[0]),
        "mix_w_out": np.ascontiguousarray(f32("mix_w_out")[0]),
        "cst": np.ascontiguousarray(cst),
    })
    WINS = (2, 4, 8, 16)
    invdiv0 = np.stack([1.0 / np.minimum(np.arange(1, 17), w) for w in WINS]).astype(np.float32).reshape(1, 64)
    invdiv1 = np.stack([np.full(16, 1.0 / w) for w in WINS]).astype(np.float32).reshape(1, 64)
    in_maps = []
    for i in range(NCORES):
        b, sh = i // 2, i % 2
        xe = np.zeros((TE, D), np.float32)
        xe[TH:] = x[b, sh * T:(sh + 1) * T]
        if sh == 1:
            xe[:TH] = x[b, T - TH:T]
        m = dict(shared)
        m["x_ext"] = xe
        m["cT"] = _layout_T(c[b], 8)
        m["halo_mask"] = np.full((P, 1), float(sh), np.float32)
        m["invdiv"] = np.ascontiguousarray(np.tile(invdiv1 if sh else invdiv0, (P, 1)))
        in_maps.append(m)
    ncr = B.debug.get("ncores", NCORES)
    res = run_bass_kernel_spmd(B.nc, in_maps[:ncr], core_ids=list(range(ncr)))
    out = np.zeros((4, 2 * T, D), np.float32)
    for i in range(ncr):
        b, sh = i // 2, i % 2
        out[b, sh * T:(sh + 1) * T] = res.results[i]["out"]
    return out
```
